# Optimizing a Trainium2 kernel written in Bass

```python
import jax
import jax.numpy as jnp
from jax import lax
import numpy as np

D_MODEL = 1024
BATCH = 8
SEQ = 2048
DEPTH = 2

MEM_LEN = 256
EPS = 1e-6
CONV_W = 4
D_RNN = D_MODEL
RNN_BLOCKS = 16
RNN_BLOCK = D_RNN // RNN_BLOCKS
LRU_C = 8.0
ML_HEADS = 4
ML_DHEAD = D_MODEL // ML_HEADS
D_ML = ML_HEADS * ML_DHEAD
ML_CHUNK = 128
XA_HEADS = 4
XA_DHEAD = D_MODEL // XA_HEADS
D_FF = 3 * D_MODEL
FFN_CONV_W = 3
IN_SPLIT_SIZES = (D_RNN, D_RNN, 2 * D_ML, D_ML, D_ML, 2 * ML_HEADS, D_MODEL, D_MODEL)
D_IN = 2 * D_RNN + 4 * D_ML + 2 * ML_HEADS + 2 * D_MODEL

kernel_name = 'hybrid_rglru_mlstm_xattn_convffn'


def rms_norm(x, g):
    xf = x.astype(jnp.float32)
    y = xf * lax.rsqrt(jnp.mean(xf * xf, axis=-1, keepdims=True) + EPS)
    return (y * g.astype(jnp.float32)).astype(x.dtype)


def causal_dwconv(x, w, b):
    k_w, s = w.shape[0], x.shape[1]
    xp = jnp.pad(x, ((0, 0), (k_w - 1, 0), (0, 0)))
    out = xp[:, 0:s] * w[0] + b
    for j in range(1, k_w):
        out = out + xp[:, j:j + s] * w[j]
    return out


def rg_lru(x, w_a, b_a, w_x, b_x, lam):
    bsz, s, _ = x.shape
    xb = x.reshape(bsz, s, RNN_BLOCKS, RNN_BLOCK)
    r = jax.nn.sigmoid(jnp.einsum('bsgi,gij->bsgj', xb, w_a) + b_a.reshape(RNN_BLOCKS, RNN_BLOCK))
    i = jax.nn.sigmoid(jnp.einsum('bsgi,gij->bsgj', xb, w_x) + b_x.reshape(RNN_BLOCKS, RNN_BLOCK))
    r = r.reshape(bsz, s, D_RNN).astype(jnp.float32)
    i = i.reshape(bsz, s, D_RNN).astype(jnp.float32)
    log_a = -LRU_C * r * jax.nn.softplus(-lam.astype(jnp.float32))
    a = jnp.exp(log_a)
    u = jnp.sqrt(-jnp.expm1(2.0 * log_a)) * (i * x.astype(jnp.float32))

    def combine(left, right):
        a1, b1 = left
        a2, b2 = right
        return a1 * a2, a2 * b1 + b2

    _, h = lax.associative_scan(combine, (a, u), axis=1)
    return h.astype(x.dtype)


def _to_chunks(t, n_chunks):
    bsz, s, h = t.shape[:3]
    t = t.reshape(bsz, n_chunks, s // n_chunks, h, *t.shape[3:])
    return jnp.moveaxis(t, (1, 3), (0, 2))


def mlstm(q, k, v, i_pre, f_pre):
    bsz, s, nh, dh = q.shape
    n_chunks = s // ML_CHUNK
    f32 = jnp.float32
    q = q.astype(f32)
    k = k.astype(f32) * (dh ** -0.5)
    v = v.astype(f32)
    log_i = i_pre.astype(f32)
    log_f = jax.nn.log_sigmoid(f_pre.astype(f32))
    xs = (_to_chunks(q, n_chunks), _to_chunks(k, n_chunks), _to_chunks(v, n_chunks),
          _to_chunks(log_i, n_chunks), _to_chunks(log_f, n_chunks))
    causal = jnp.tril(jnp.ones((ML_CHUNK, ML_CHUNK), dtype=bool))

    def step(carry, inp):
        c_st, n_st, m_st = carry
        qc, kc, vc, li, lf = inp
        b = jnp.cumsum(lf, axis=-1)
        a_inter = b + m_st[..., None]
        d = b[..., :, None] - b[..., None, :] + li[..., None, :]
        d = jnp.where(causal, d, -jnp.inf)
        m_t = jnp.maximum(a_inter, jnp.max(d, axis=-1))
        w_inter = jnp.exp(a_inter - m_t)
        sc = jnp.einsum('bhtd,bhsd->bhts', qc, kc) * jnp.exp(d - m_t[..., None])
        num = (w_inter[..., None] * jnp.einsum('bhtd,bhde->bhte', qc, c_st)
               + jnp.einsum('bhts,bhse->bhte', sc, vc))
        den = w_inter * jnp.einsum('bhtd,bhd->bht', qc, n_st) + jnp.sum(sc, axis=-1)
        h = num / jnp.maximum(jnp.abs(den), jnp.exp(-m_t))[..., None]
        g = b[..., -1]
        u = g[..., None] - b + li
        m_next = jnp.maximum(g + m_st, jnp.max(u, axis=-1))
        decay = jnp.exp(g + m_st - m_next)
        wk = kc * jnp.exp(u - m_next[..., None])[..., None]
        c_next = decay[..., None, None] * c_st + jnp.einsum('bhsd,bhse->bhde', wk, vc)
        n_next = decay[..., None] * n_st + jnp.sum(wk, axis=2)
        return (c_next, n_next, m_next), h

    init = (jnp.zeros((bsz, nh, dh, dh), f32), jnp.zeros((bsz, nh, dh), f32),
            jnp.zeros((bsz, nh), f32))
    _, hs = lax.scan(step, init, xs)
    hs = jnp.moveaxis(hs, (0, 2), (1, 3))
    return hs.reshape(bsz, s, nh * dh)


def head_norm(h, g, n_heads):
    bsz, s, w = h.shape
    hh = h.reshape(bsz, s, n_heads, w // n_heads)
    mu = jnp.mean(hh, axis=-1, keepdims=True)
    var = jnp.mean(jnp.square(hh - mu), axis=-1, keepdims=True)
    hh = (hh - mu) * lax.rsqrt(var + EPS)
    return hh.reshape(bsz, s, w) * g.astype(jnp.float32)


def hybrid_mixer(xn, w_in, rnn_conv_w, rnn_conv_b, lru_wa, lru_ba, lru_wx, lru_bx, lru_lambda,
                 ml_conv_w, ml_conv_b, ml_if_b, ml_norm_g, w_branch_a, w_branch_b, w_mix_out):
    bsz, s, _ = xn.shape
    proj = xn @ w_in
    idx = np.cumsum(IN_SPLIT_SIZES)[:-1].tolist()
    xr, gr, qk, v, o, ifg, ga, gb = jnp.split(proj, idx, axis=-1)
    xr = causal_dwconv(xr, rnn_conv_w, rnn_conv_b)
    ya = jax.nn.gelu(gr) * rg_lru(xr, lru_wa, lru_ba, lru_wx, lru_bx, lru_lambda)
    qk = jax.nn.silu(causal_dwconv(qk, ml_conv_w, ml_conv_b))
    q, k = jnp.split(qk, 2, axis=-1)
    ifg = ifg + ml_if_b
    i_pre, f_pre = jnp.split(ifg, 2, axis=-1)
    shp = (bsz, s, ML_HEADS, ML_DHEAD)
    hm = mlstm(q.reshape(shp), k.reshape(shp), v.reshape(shp), i_pre, f_pre)
    yb = (jax.nn.sigmoid(o.astype(jnp.float32)) * head_norm(hm, ml_norm_g, ML_HEADS)).astype(xn.dtype)
    y = jax.nn.sigmoid(ga) * (ya @ w_branch_a) + jax.nn.sigmoid(gb) * (yb @ w_branch_b)
    return y @ w_mix_out


def cross_attention(xn, memn, w_q, w_kv, w_o):
    bsz, s, _ = xn.shape
    m = memn.shape[1]
    q = (xn @ w_q).reshape(bsz, s, XA_HEADS, XA_DHEAD)
    k, v = jnp.split(memn @ w_kv, 2, axis=-1)
    k = k.reshape(bsz, m, XA_HEADS, XA_DHEAD)
    v = v.reshape(bsz, m, XA_HEADS, XA_DHEAD)
    sc = jnp.einsum('bshd,bmhd->bhsm', q, k).astype(jnp.float32) * (XA_DHEAD ** -0.5)
    p = jax.nn.softmax(sc, axis=-1).astype(v.dtype)
    out = jnp.einsum('bhsm,bmhd->bshd', p, v).reshape(bsz, s, XA_HEADS * XA_DHEAD)
    return out @ w_o


def conv_ffn(xn, w_up, conv_w, conv_b, w_down):
    h = causal_dwconv(xn @ w_up, conv_w, conv_b)
    g, u = jnp.split(h, 2, axis=-1)
    return (jax.nn.gelu(g) * u) @ w_down


def setup_inputs(seed: int = 0) -> dict:
    key = jax.random.key(seed)
    ks = iter(jax.random.split(key, 40))
    f32 = jnp.float32

    def nrm(shape, scale):
        return jax.random.normal(next(ks), shape, f32) * scale

    def gain(shape):
        return 1.0 + nrm(shape, 0.02)

    x = nrm((BATCH, SEQ, D_MODEL), 1.0)
    mem = nrm((BATCH, MEM_LEN, D_MODEL), 1.0)
    norm_mix_g = gain((DEPTH, D_MODEL))
    w_in = nrm((DEPTH, D_MODEL, D_IN), D_MODEL ** -0.5)
    rnn_conv_w = nrm((DEPTH, CONV_W, D_RNN), CONV_W ** -0.5)
    rnn_conv_b = nrm((DEPTH, D_RNN), 0.01)
    lru_wa = nrm((DEPTH, RNN_BLOCKS, RNN_BLOCK, RNN_BLOCK), RNN_BLOCK ** -0.5)
    lru_ba = nrm((DEPTH, D_RNN), 0.01)
    lru_wx = nrm((DEPTH, RNN_BLOCKS, RNN_BLOCK, RNN_BLOCK), RNN_BLOCK ** -0.5)
    lru_bx = nrm((DEPTH, D_RNN), 0.01)
    a_c = jax.random.uniform(next(ks), (DEPTH, D_RNN), f32, 0.9, 0.999)
    a0 = a_c ** (1.0 / LRU_C)
    lru_lambda = jnp.log(a0) - jnp.log1p(-a0)
    ml_conv_w = nrm((DEPTH, CONV_W, 2 * D_ML), CONV_W ** -0.5)
    ml_conv_b = nrm((DEPTH, 2 * D_ML), 0.01)
    i_bias = nrm((DEPTH, ML_HEADS), 0.1)
    f_bias = jnp.linspace(3.0, 6.0, ML_HEADS, dtype=f32)[None, :] + nrm((DEPTH, ML_HEADS), 0.1)
    ml_if_b = jnp.concatenate([i_bias, f_bias], axis=-1)
    ml_norm_g = gain((DEPTH, D_ML))
    w_branch_a = nrm((DEPTH, D_RNN, D_MODEL), D_RNN ** -0.5)
    w_branch_b = nrm((DEPTH, D_ML, D_MODEL), D_ML ** -0.5)
    w_mix_out = nrm((DEPTH, D_MODEL, D_MODEL), D_MODEL ** -0.5)
    norm_xa_g = gain((DEPTH, D_MODEL))
    xa_wq = nrm((DEPTH, D_MODEL, XA_HEADS * XA_DHEAD), D_MODEL ** -0.5)
    xa_wkv = nrm((DEPTH, D_MODEL, 2 * XA_HEADS * XA_DHEAD), D_MODEL ** -0.5)
    xa_wo = nrm((DEPTH, XA_HEADS * XA_DHEAD, D_MODEL), (XA_HEADS * XA_DHEAD) ** -0.5)
    norm_ffn_g = gain((DEPTH, D_MODEL))
    ffn_w_up = nrm((DEPTH, D_MODEL, 2 * D_FF), D_MODEL ** -0.5)
    ffn_conv_w = nrm((DEPTH, FFN_CONV_W, 2 * D_FF), FFN_CONV_W ** -0.5)
    ffn_conv_b = nrm((DEPTH, 2 * D_FF), 0.01)
    ffn_w_down = nrm((DEPTH, D_FF, D_MODEL), D_FF ** -0.5)
    mem_norm_g = gain((D_MODEL,))
    final_norm_g = gain((D_MODEL,))
    return {'x': x, 'mem': mem, 'norm_mix_g': norm_mix_g, 'w_in': w_in,
            'rnn_conv_w': rnn_conv_w, 'rnn_conv_b': rnn_conv_b,
            'lru_wa': lru_wa, 'lru_ba': lru_ba, 'lru_wx': lru_wx, 'lru_bx': lru_bx,
            'lru_lambda': lru_lambda, 'ml_conv_w': ml_conv_w, 'ml_conv_b': ml_conv_b,
            'ml_if_b': ml_if_b, 'ml_norm_g': ml_norm_g, 'w_branch_a': w_branch_a,
            'w_branch_b': w_branch_b, 'w_mix_out': w_mix_out, 'norm_xa_g': norm_xa_g,
            'xa_wq': xa_wq, 'xa_wkv': xa_wkv, 'xa_wo': xa_wo, 'norm_ffn_g': norm_ffn_g,
            'ffn_w_up': ffn_w_up, 'ffn_conv_w': ffn_conv_w, 'ffn_conv_b': ffn_conv_b,
            'ffn_w_down': ffn_w_down, 'mem_norm_g': mem_norm_g, 'final_norm_g': final_norm_g}


def reference(x, mem, norm_mix_g, w_in, rnn_conv_w, rnn_conv_b, lru_wa, lru_ba, lru_wx, lru_bx,
              lru_lambda, ml_conv_w, ml_conv_b, ml_if_b, ml_norm_g, w_branch_a, w_branch_b,
              w_mix_out, norm_xa_g, xa_wq, xa_wkv, xa_wo, norm_ffn_g, ffn_w_up, ffn_conv_w,
              ffn_conv_b, ffn_w_down, mem_norm_g, final_norm_g):
    memn = rms_norm(mem, mem_norm_g)
    h = x
    for l in range(DEPTH):
        h = h + hybrid_mixer(rms_norm(h, norm_mix_g[l]), w_in[l], rnn_conv_w[l], rnn_conv_b[l],
                             lru_wa[l], lru_ba[l], lru_wx[l], lru_bx[l], lru_lambda[l],
                             ml_conv_w[l], ml_conv_b[l], ml_if_b[l], ml_norm_g[l],
                             w_branch_a[l], w_branch_b[l], w_mix_out[l])
        h = h + cross_attention(rms_norm(h, norm_xa_g[l]), memn, xa_wq[l], xa_wkv[l], xa_wo[l])
        h = h + conv_ffn(rms_norm(h, norm_ffn_g[l]), ffn_w_up[l], ffn_conv_w[l], ffn_conv_b[l],
                         ffn_w_down[l])
    return rms_norm(h, final_norm_g)
```

```python
import contextlib
import numpy as np
import concourse.bass as bass
import concourse.mybir as mybir
from concourse.bass_utils import run_bass_kernel_spmd

F32 = mybir.dt.float32
BF16 = mybir.dt.bfloat16
AF = mybir.ActivationFunctionType
ALU = mybir.AluOpType
AX = mybir.AxisListType


def _esize(dt):
    return mybir.dt.size(dt)


class Op:
    __slots__ = ("eng", "fn", "deps", "idx", "signal", "val", "is_dma", "dkey", "nm", "cost", "alldeps", "fcls")

    def __init__(self, eng, fn, idx, is_dma=False, dkey=None, nm=""):
        self.eng = eng
        self.fn = fn
        self.idx = idx
        self.deps = {}
        self.signal = False
        self.val = 0
        self.is_dma = is_dma
        self.dkey = dkey
        self.nm = nm
        self.cost = 100.0
        self.alldeps = None
        self.fcls = None


class Prog:
    ENGS = ("pe", "act", "dve", "pool", "sp")

    def __init__(self, nc):
        self.nc = nc
        self.ops = []
        self.recs = {}
        self.dma_cnt = {}
        self.sb_lo = 16512
        self.sb_hi = 229344
        self.names = 0

    def sbuf(self, name, shape, dtype, offset):
        self.names += 1
        nbytes = int(np.prod(shape[1:])) * _esize(dtype)
        assert offset % 32 == 0, (name, offset)
        assert self.sb_lo + offset + nbytes <= self.sb_hi, (name, offset, nbytes)
        return self.nc.alloc_sbuf_tensor_at(
            f"{name}_{self.names}", list(shape), dtype, offset=self.sb_lo + offset)

    def _foot(self, ap):
        t = ap.tensor
        es = _esize(ap.dtype)
        dims = ap.ap
        space = str(ap.space)
        off = ap.offset
        if "SB" in space or "PSUM" in space:
            pstep, pcnt = dims[0]
            if pstep > 0:
                p0 = off // pstep
                foff = off % pstep
            else:
                p0 = 0
                foff = off
            if "SB" in space:
                base = t.manual_sbuf_range[0]
                key = "SB"
            else:
                base = 0
                key = "PS"
            lo = base + foff * es
            if key == "PS":
                hi = lo
                for s_, c_ in dims[1:]:
                    hi += (c_ - 1) * abs(s_) * es
                b_lo = lo // 2048
                b_hi = hi // 2048
                return [("PS", 0, 128, b * 2048, (b + 1) * 2048) for b in range(b_lo, b_hi + 1)]
            free = [d for d in dims[1:] if d[1] > 1]
            if not free:
                return [(key, p0, p0 + pcnt, lo, lo + es)]
            inner = free[-1]
            ilen = ((inner[1] - 1) * abs(inner[0]) + 1) * es
            outer = free[:-1]
            nout = 1
            for s, c in outer:
                nout *= c
            if nout > 64 or any(s < 0 for s, c in free):
                hi = lo
                for s, c in free:
                    hi += (c - 1) * abs(s) * es
                return [(key, p0, p0 + pcnt, lo, hi + es)]
            res = []
            idxs = [0] * len(outer)
            while True:
                o = lo
                for (s, c), i in zip(outer, idxs):
                    o += i * s * es
                res.append((key, p0, p0 + pcnt, o, o + ilen))
                k = len(outer) - 1
                while k >= 0:
                    idxs[k] += 1
                    if idxs[k] < outer[k][1]:
                        break
                    idxs[k] = 0
                    k -= 1
                if k < 0:
                    break
            return res
        else:
            lo = off * es
            hi = lo
            for s, c in dims:
                hi += (c - 1) * abs(s) * es
            return [("D:" + t.name, 0, 1, lo, hi + es)]

    def add(self, eng, fn, reads=(), writes=(), dkey=None, nm=""):
        idx = len(self.ops)
        is_dma = dkey is not None
        op = Op(eng, fn, idx, is_dma, dkey, nm)
        ekey = ("dma", idx) if is_dma else eng
        newrecs = []
        for ap in reads:
            for (key, p0, p1, lo, hi) in self._foot(ap):
                d = self.recs.setdefault(key, {})
                if key == "PS":
                    dead = []
                    for rk, oi in d.items():
                        (q0, q1, l2, h2, ek, isw, tw) = rk
                        if p0 < q1 and q0 < p1 and lo < h2 and l2 < hi:
                            if tw:
                                op.deps[oi] = "raw"
                            elif oi not in op.deps:
                                op.deps[oi] = "war"
                            dead.append(rk)
                    for rk in dead:
                        del d[rk]
                    newrecs.append((key, (p0, p1, lo, hi, ekey, True, False)))
                    continue
                for (q0, q1, l2, h2, ek, isw, tw), oi in d.items():
                    if isw and p0 < q1 and q0 < p1 and lo < h2 and l2 < hi:
                        op.deps[oi] = "raw"
                newrecs.append((key, (p0, p1, lo, hi, ekey, False, False)))
        for ap in writes:
            for (key, p0, p1, lo, hi) in self._foot(ap):
                d = self.recs.setdefault(key, {})
                dead = []
                for rk, oi in d.items():
                    (q0, q1, l2, h2, ek, isw, tw) = rk
                    if p0 < q1 and q0 < p1 and lo < h2 and l2 < hi:
                        if oi not in op.deps:
                            op.deps[oi] = "waw" if tw else "war"
                        if p0 <= q0 and q1 <= p1 and lo <= l2 and h2 <= hi:
                            dead.append(rk)
                for rk in dead:
                    del d[rk]
                newrecs.append((key, (p0, p1, lo, hi, ekey, True, True)))
        for key, rk in newrecs:
            prev = self.recs[key].get(rk)
            if prev is not None and prev != idx and prev not in op.deps:
                op.deps[prev] = "ord"
            self.recs[key][rk] = idx
        if is_dma:
            self.dma_cnt[dkey] = self.dma_cnt.get(dkey, 0) + 16
            op.val = self.dma_cnt[dkey]
        try:
            if eng == "pe":
                n = reads[1].free_size() if len(reads) > 1 else 128
                op.cost = max(64.0, float(n)) / 2.0 + 12.0
            elif is_dma:
                a = writes[0]
                op.cost = 2000.0 + a.partition_size() * a.free_size() * _esize(a.dtype) / 150.0
            elif writes:
                a = writes[0]
                op.cost = 120.0 + a.free_size() / 0.96
                if reads and "PSUM" in str(reads[0].space):
                    op.cost += 60.0
                if nm == "scan":
                    op.cost += a.free_size() / 0.96
        except Exception:
            pass
        self.ops.append(op)
        return op

    def mm(self, out, lhsT, rhs, start=True, stop=True):
        return self.add("pe", lambda e: e.matmul(out, lhsT, rhs, start=start, stop=stop),
                        reads=[lhsT, rhs], writes=[out])

    def tr(self, out, in_, ident):
        return self.add("pe", lambda e: e.transpose(out, in_, ident),
                        reads=[in_, ident], writes=[out])

    def act(self, out, in_, func, bias=None, scale=None, eng="act"):
        kw = {}
        rd = [in_]
        if bias is not None:
            kw["bias"] = bias
            if not isinstance(bias, (int, float)):
                rd.append(bias)
        if scale is not None:
            kw["scale"] = scale
            if not isinstance(scale, (int, float)):
                rd.append(scale)
        op = self.add(eng, lambda e: e.activation(out, in_, func, **kw), reads=rd, writes=[out])
        if func in (AF.Sigmoid, AF.Silu, AF.Sqrt):
            op.fcls = str(func)
        elif func in (AF.Exp, AF.Ln):
            op.fcls = "explog"
        return op

    def ts(self, out, in0, s1, s2, op0, op1=None, eng="dve"):
        rd = [in0]
        for s in (s1, s2):
            if s is not None and not isinstance(s, (int, float)):
                rd.append(s)
        if op1 is None:
            return self.add(eng, lambda e: e.tensor_scalar(out, in0, s1, None, op0), reads=rd, writes=[out])
        return self.add(eng, lambda e: e.tensor_scalar(out, in0, s1, s2, op0, op1), reads=rd, writes=[out])

    def tt(self, out, in0, in1, op, eng="dve"):
        return self.add(eng, lambda e: e.tensor_tensor(out, in0, in1, op), reads=[in0, in1], writes=[out])

    def stt(self, out, in0, scalar, in1, op0, op1, eng="dve"):
        rd = [in0, in1]
        if not isinstance(scalar, (int, float)):
            rd.append(scalar)
        return self.add(eng, lambda e: e.scalar_tensor_tensor(out, in0, scalar, in1, op0, op1),
                        reads=rd, writes=[out])

    def copy(self, out, in_, eng="dve"):
        if eng == "act":
            return self.act(out, in_, AF.Copy)
        return self.add(eng, lambda e: e.tensor_copy(out, in_), reads=[in_], writes=[out])

    def memset(self, out, val, eng="dve"):
        return self.add(eng, lambda e: e.memset(out, val), reads=[], writes=[out])

    def scan(self, out, d0, d1, initial, op0, op1):
        rd = [d0, d1]
        if not isinstance(initial, (int, float)):
            rd.append(initial)
        return self.add("dve", lambda e: e.tensor_tensor_scan(out, d0, d1, initial, op0, op1),
                        reads=rd, writes=[out], nm="scan")

    def dma(self, q, out, in_, dkey):
        return self.add(q, lambda e: e.dma_start(out=out, in_=in_), reads=[in_], writes=[out], dkey=dkey)

    def finish(self, eng, aps):
        return self.add(eng, None, reads=list(aps), writes=[])

    def schedule(self):
        import heapq
        ops = self.ops
        n = len(ops)
        succ = [[] for _ in range(n)]
        indeg = [0] * n
        for op in ops:
            op.alldeps = dict(op.deps)
            for d in op.deps:
                succ[d].append(op.idx)
                indeg[op.idx] += 1
        fin = [0.0] * n
        ready = [0.0] * n
        tail = [0.0] * n
        for op in reversed(ops):
            m_ = 0.0
            for sidx in succ[op.idx]:
                if tail[sidx] > m_:
                    m_ = tail[sidx]
            tail[op.idx] = op.cost + 300.0 + m_
        released = {e: [] for e in self.ENGS}
        for op in ops:
            if indeg[op.idx] == 0:
                released[op.eng].append(op.idx)
        tnow = {e: 0.0 for e in self.ENGS}
        order = {e: [] for e in self.ENGS}
        last_tbl = [None]
        remaining = n
        active = set(self.ENGS)
        while remaining > 0:
            cand = [e for e in self.ENGS if released[e]]
            assert cand, "scheduler deadlock"
            e = min(cand, key=lambda k: tnow[k])
            t = tnow[e]
            rl = released[e]
            best = None
            bk = None
            for i in rl:
                pen = 0.0
                if e == "act":
                    fc_ = ops[i].fcls
                    if fc_ is not None and fc_ != last_tbl[0]:
                        pen = 1300.0
                k = (int((max(ready[i], t) + pen) // 250.0), -tail[i], i)
                if bk is None or k < bk:
                    bk = k
                    best = i
            rl.remove(best)
            op = ops[best]
            st_ = max(ready[best], t)
            if e == "act" and op.fcls is not None:
                if op.fcls != last_tbl[0]:
                    st_ += 1300.0
                last_tbl[0] = op.fcls
            if op.is_dma:
                issue = 1500.0 if e == "pool" else 100.0
                tnow[e] = st_ + issue
                fin[best] = st_ + op.cost
            elif op.fn is None:
                tnow[e] = st_
                fin[best] = st_
            else:
                tnow[e] = st_ + op.cost
                fin[best] = st_ + op.cost + (60.0 if e == "pe" else 350.0)
            order[e].append(op)
            remaining -= 1
            for sidx in succ[best]:
                indeg[sidx] -= 1
                if fin[best] > ready[sidx]:
                    ready[sidx] = fin[best]
                if indeg[sidx] == 0:
                    released[ops[sidx].eng].append(sidx)
        self.est_ns = max(tnow.values())
        return order

    def emit(self, sched=True):
        nc = self.nc
        ops = self.ops
        if sched:
            per = self.schedule()
        else:
            per = {e: [op for op in ops if op.eng == e] for e in self.ENGS}
        for op in ops:
            need = {}
            for di, kind in op.deps.items():
                p = ops[di]
                if p.is_dma:
                    need[di] = kind
                elif p.eng == op.eng and not op.is_dma:
                    if op.eng == "pe":
                        continue
                    if kind in ("raw", "war", "waw"):
                        need[di] = kind
                else:
                    need[di] = kind
            op.deps = need
            for di in need:
                if not ops[di].is_dma:
                    ops[di].signal = True
        cnt = {e: 0 for e in self.ENGS}
        for e in self.ENGS:
            for op in per[e]:
                if not op.is_dma and op.signal:
                    cnt[e] += 1
                    op.val = cnt[e]
        with contextlib.ExitStack() as st:
            sems = {}
            for e in self.ENGS:
                sems[e] = st.enter_context(nc.semaphore("s_" + e))
            for k in self.dma_cnt:
                sems[("d", k)] = st.enter_context(nc.semaphore("d_" + str(k)))
            def run(ename, eh):
                waited = {}
                for op in per[ename]:
                    w = {}
                    for di in op.deps:
                        p = ops[di]
                        sk = ("d", p.dkey) if p.is_dma else p.eng
                        if p.val > w.get(sk, 0):
                            w[sk] = p.val
                    for sk, v in w.items():
                        if v > waited.get(sk, 0):
                            eh.wait_ge(sems[sk], v)
                            waited[sk] = v
                    if op.fn is None:
                        continue
                    ins = op.fn(eh)
                    if op.is_dma:
                        ins.then_inc(sems[("d", op.dkey)], 16)
                    elif op.signal:
                        ins.then_inc(sems[ename], 1)

            with nc.Block() as block:
                @block.tensor
                def _(e):
                    run("pe", e)

                @block.scalar
                def _(e):
                    run("act", e)

                @block.vector
                def _(e):
                    run("dve", e)

                @block.gpsimd
                def _(e):
                    run("pool", e)

                @block.sync
                def _(e):
                    run("sp", e)
        return cnt


S = 2048
D = 1024
L = 2
MEM = 256
NP = 384
EPS = 1e-6
LN16 = float(np.log(16.0))
FFN_LAST = "pool"
GC = 1.5957691216057308


def bc_last(ap, n):
    return bass.AP(ap.tensor, ap.offset, [list(d) for d in ap.ap] + [[0, n]])


class Arena:
    def __init__(self, p):
        self.p = p
        self.off = 0

    def alloc(self, name, shape, dtype):
        nbytes = int(np.prod(shape[1:])) * _esize(dtype)
        o = self.off
        self.off = (o + nbytes + 31) // 32 * 32
        return self.p.sbuf(name, shape, dtype, o)

    def mark(self):
        return self.off

    def reset(self, m):
        self.off = m


class WPool:
    def __init__(self, p, ar, name, kc, ncols, nslots=4):
        self.p = p
        self.kc = kc
        self.ncols = ncols
        self.bf = [ar.alloc(f"{name}_bf{i}", [128, kc, ncols], BF16) for i in range(nslots)]
        self.name = name
        self.n = 0

    def load(self, src_tile, dst=None, key=None):
        p = self.p
        if dst is None:
            i = self.n
            self.n += 1
            dst = self.bf[i % len(self.bf)][:]
            key = f"{self.name}{i % len(self.bf)}"
        kc = dst.shape[1]
        p.dma("pool", dst, src_tile.rearrange("p (k n) -> p k n", k=kc), key)
        return dst


def build(debug=None):
    nc = bass.Bass("TRN2", target_bir_lowering=False)
    dt_in = lambda n, s: nc.dram_tensor(n, list(s), F32, kind="ExternalInput").ap()
    x_d = dt_in("x", [S, D])
    mem_d = dt_in("mem", [MEM, D])
    const_d = dt_in("consts", [128, 768])
    pvec_d = dt_in("pvec", [L, 128, NP])
    gb_d = dt_in("gbc", [L, 128, D])
    bd_d = dt_in("bd", [L, 2, 128, 8 * 128])
    win_d = dt_in("w_in_t", [L, 64, 128, 8 * 128])
    wif_d = dt_in("w_if", [L, 128, 8 * 8])
    wba_d = dt_in("w_ba", [L, 8, 128, 8 * 128])
    wbb_d = dt_in("w_bb", [L, 8, 128, 8 * 128])
    wmo_d = dt_in("w_mo", [L, 8, 128, 8 * 128])
    wq_d = dt_in("xa_wq", [L, 8, 128, 8 * 128])
    wkv_d = dt_in("xa_wkv", [L, 16, 128, 8 * 128])
    wo_d = dt_in("xa_wo", [L, 8, 128, 8 * 128])
    wup_d = dt_in("ffn_up", [L, 48, 128, 8 * 128])
    wdn_d = dt_in("ffn_dn", [L, 8, 128, 24 * 128])
    out_d = nc.dram_tensor("out", [S, D], F32, kind="ExternalOutput").ap()
    y1_d = nc.dram_tensor("y1s", [8, 128, S], BF16, kind="Internal").ap()
    dbg_d = None
    if debug is not None:
        dbg_d = nc.dram_tensor("dbg", [128, 8 * 2048], F32, kind="ExternalOutput").ap()

    p = Prog(nc)
    ar = Arena(p)
    ps = nc.alloc_psum_tensor("ps", [128, 8, 512], F32)

    def bank(i):
        return ps[:, i % 8, :]

    def bankb(i):
        return ps[:, i % 8, :].bitcast(BF16)

    H = ar.alloc("H", [128, 8, S], F32)
    XN = ar.alloc("XN", [128, 8, S], BF16)
    CF = ar.alloc("CF", [128, 768], F32)
    identb = ar.alloc("identb", [128, 128], BF16)
    onesb = ar.alloc("onesb", [128, 128], BF16)
    selb = ar.alloc("selb", [4, 512], BF16)
    PV = ar.alloc("PV", [128, NP], F32)
    GB = ar.alloc("GB", [128, D], F32)
    BDb = ar.alloc("BDb", [128, 2, 8, 128], BF16)
    MT = ar.alloc("memnT", [128, 8, MEM], BF16)
    CL = ar.alloc("CL", [128, 16], F32)
    identf = CF[:, 0:128]
    maskf = CF[:, 128:256]
    base = ar.mark()

    def sl(t, n=512):
        return slice(t * n, (t + 1) * n)

    p.dma("sp", CF[:], const_d, "c0")
    p.copy(identb[:], CF[:, 0:128])
    p.memset(onesb[:], 1.0)
    p.copy(selb[:], CF[0:4, 256:768])

    m0 = ar.mark()
    xst = [ar.alloc(f"xst{i}", [128, D], F32) for i in range(2)]
    for t in range(16):
        st = xst[t % 2]
        p.dma("sp", st[:], x_d[t * 128:(t + 1) * 128, :], f"xst{t % 2}")
        for b in range(2):
            for k in range(4):
                kc = b * 4 + k
                p.tr(ps[:, (t * 2 + b) % 8, k * 128:(k + 1) * 128], st[:, kc * 128:(kc + 1) * 128], identf)
            src = ps[:, (t * 2 + b) % 8, :].rearrange("p (k n) -> p k n", k=4)
            p.copy(H[:, b * 4:(b + 1) * 4, t * 128:(t + 1) * 128], src, eng="act" if b else "dve")

    def load_params(l):
        p.dma("sp", PV[:], pvec_d[l], "pv")
        p.dma("sp", GB[:], gb_d[l], "gb")
        bst = xst[0]
        for g in range(2):
            p.dma("sp", bst[:], bd_d[l, g], "bdst")
            p.copy(BDb[:, g, :, :], bst[:].rearrange("p (k n) -> p k n", k=8))
        tmp = xst[1]
        p.act(tmp[:, 0:8], PV[:, 96:104], AF.Exp, scale=-1.0)
        p.act(tmp[:, 8:16], tmp[:, 0:8], AF.Ln, bias=1.0)
        p.ts(CL[:, 0:8], tmp[:, 8:16], -8.0, None, ALU.mult)
        p.ts(CL[:, 8:16], tmp[:, 8:16], -16.0, None, ALU.mult)

    load_params(0)
    mst = xst
    scr = ar.alloc("mscr", [128, D], F32)
    small = ar.alloc("msm", [128, 8], F32)
    for mc in range(2):
        st = mst[mc]
        p.dma("sp", st[:], mem_d[mc * 128:(mc + 1) * 128, :], f"xst{mc}")
        p.tt(scr[:], st[:], st[:], ALU.mult)
        p.add("dve", lambda e, o=small[:, 0:1], i=scr[:]: e.tensor_reduce(o, i, AX.X, ALU.add),
              reads=[scr[:]], writes=[small[:, 0:1]])
        p.act(small[:, 1:2], small[:, 0:1], AF.Sqrt, bias=EPS, scale=1.0 / D)
        p.add("dve", lambda e, o=small[:, 2:3], i=small[:, 1:2]: e.reciprocal(o, i),
              reads=[small[:, 1:2]], writes=[small[:, 2:3]])
        p.ts(scr[:], st[:], small[:, 2:3], None, ALU.mult)
        for b in range(2):
            for k in range(4):
                kc = b * 4 + k
                p.tr(ps[:, b, k * 128:(k + 1) * 128], scr[:, kc * 128:(kc + 1) * 128], identf)
            for k in range(4):
                kc = b * 4 + k
                p.ts(MT[:, kc, mc * 128:(mc + 1) * 128], ps[:, b, k * 128:(k + 1) * 128],
                     PV[:, 32 + kc:33 + kc], None, ALU.mult)
    ar.reset(m0)

    def rmsnorm(gcol0):
        m = ar.mark()
        sq = [ar.alloc(f"sq{i}", [128, 512], BF16) for i in range(2)]
        rt = ar.alloc("rt", [128, 512], F32)
        rs = ar.alloc("rs", [128, 512], F32)
        for t in range(4):
            for kc in range(8):
                s_ = sq[kc % 2]
                p.act(s_[:], H[:, kc, sl(t)], AF.Square)
                p.mm(bank(t), onesb[:], s_[:], start=(kc == 0), stop=(kc == 7))
            p.act(rt[:], bank(t), AF.Sqrt, bias=EPS, scale=1.0 / D)
            p.add("dve", lambda e, o=rs[:], i=rt[:]: e.reciprocal(o, i), reads=[rt[:]], writes=[rs[:]])
            for kc in range(8):
                p.stt(XN[:, kc, sl(t)], H[:, kc, sl(t)], PV[:, gcol0 + kc:gcol0 + kc + 1], rs[:],
                      ALU.mult, ALU.mult)
        ar.reset(m)

    def gelu_mul(out, xin, other, tmp, last_eng="dve", mid_eng="dve"):
        t1 = tmp
        p.act(t1, xin, AF.Square, scale=float(np.sqrt(0.044715)))
        p.stt(t1, t1, 1.0, xin, ALU.add, ALU.mult)
        p.act(t1, t1, AF.Sigmoid, scale=GC)
        p.tt(t1, t1, xin, ALU.mult, eng=mid_eng)
        p.tt(out, t1, other, ALU.mult, eng=last_eng)

    def conv_chunk(psb, xs_pp, par, first, nt, wcol0, bcol, ntap, acc):
        hl = ntap - 1
        xs = xs_pp[par]
        p.copy(xs[:, hl:hl + nt], psb, eng="act")
        if first:
            p.memset(xs[:, 0:hl], 0.0)
        else:
            p.copy(xs[:, 0:hl], xs_pp[1 - par][:, nt:nt + hl])
        p.act(acc, psb, AF.Identity, bias=PV[:, bcol:bcol + 1], scale=PV[:, wcol0 + hl:wcol0 + hl + 1])
        for j in range(hl):
            p.stt(acc, xs[:, j:j + nt], PV[:, wcol0 + j:wcol0 + j + 1], acc, ALU.mult, ALU.add)

    def proj_merge(l, src, wb_d, gtile0, first):
        m = ar.mark()
        Y = ar.alloc("Ymerge", [128, 8, S], BF16)
        wp = WPool(p, ar, "wm", 8, 128, 4)
        sg = [ar.alloc(f"sg{i}", [128, 512], F32) for i in range(2)]
        y1b = None
        if not first:
            y1b = [ar.alloc(f"y1b{i}", [128, S], BF16) for i in range(2)]
        for j in range(8):
            wb = wp.load(wb_d[l, j])
            wg = wp.load(win_d[l, gtile0 + j])
            if not first:
                p.dma("sp", y1b[j % 2][:], y1_d[j], f"y1b{j % 2}")
            for t in range(4):
                b0 = (t % 2) * 2
                for kc in range(8):
                    p.mm(bank(b0), wb[:, kc, :], src[:, kc, sl(t)], start=(kc == 0), stop=(kc == 7))
                for kc in range(8):
                    p.mm(bank(b0 + 1), wg[:, kc, :], XN[:, kc, sl(t)], start=(kc == 0), stop=(kc == 7))
                s_ = sg[t % 2]
                p.act(s_[:], bank(b0 + 1), AF.Sigmoid)
                if first:
                    p.tt(Y[:, j, sl(t)], bank(b0), s_[:], ALU.mult)
                else:
                    p.tt(s_[:], bank(b0), s_[:], ALU.mult)
                    p.tt(Y[:, j, sl(t)], s_[:], y1b[j % 2][:, sl(t)], ALU.add)
            if first:
                p.dma("sp", y1_d[j], Y[:, j, :], f"y1w{j}")
        if not first:
            for j in range(8):
                wm = wp.load(wmo_d[l, j])
                for t in range(4):
                    b0 = 4 + (t % 2)
                    for kc in range(8):
                        p.mm(bank(b0), wm[:, kc, :], Y[:, kc, sl(t)], start=(kc == 0), stop=(kc == 7))
                    p.tt(H[:, j, sl(t)], H[:, j, sl(t)], bank(b0), ALU.add)
        ar.reset(m)

    def out_proj(l, src, w_d, wp):
        for j in range(8):
            wm = wp.load(w_d[l, j])
            for t in range(4):
                b0 = 4 + (t % 2)
                for kc in range(8):
                    p.mm(bank(b0), wm[:, kc, :], src[:, kc, sl(t)], start=(kc == 0), stop=(kc == 7))
                p.tt(H[:, j, sl(t)], H[:, j, sl(t)], bank(b0), ALU.add)

    def dump(ap3):
        shp = ap3.shape
        n = int(np.prod(shp[1:]))
        m = ar.mark()
        p.dma("sp", dbg_d[:, 0:n], ap3, "dbg") if ap3.dtype == F32 else None
        ar.reset(m)

    for l in range(L):
        if l > 0:
            load_params(l)
        rmsnorm(0)
        if debug == "xn" and l == 0:
            break
        mA = ar.mark()
        YA = ar.alloc("YA", [128, 8, S], BF16)
        wp = WPool(p, ar, "wa", 8, 128, 4)
        xs_pp = [ar.alloc(f"xs{i}", [128, 515], F32) for i in range(2)]
        def dbl(nm, dt_=F32):
            return [ar.alloc(f"{nm}{i}", [128, 512], dt_) for i in range(2)]
        xc2, xcb2, rr2, ii2, aa2, a22, xg2, g12 = dbl("xc"), dbl("xcb", BF16), dbl("rr"), dbl("ii"), dbl("aa"), dbl("a2"), dbl("xg"), dbl("g1")
        hs = [ar.alloc(f"hs{i}", [128, 512], F32) for i in range(2)]
        for fc in range(8):
            wx = wp.load(win_d[l, fc])
            wg = wp.load(win_d[l, 8 + fc])
            for t in range(4):
                b0 = (t % 2) * 4
                xc, xcb, rr, ii, aa, a2, xg, g1 = (v[t % 2] for v in (xc2, xcb2, rr2, ii2, aa2, a22, xg2, g12))
                for kc in range(8):
                    p.mm(bank(b0), wx[:, kc, :], XN[:, kc, sl(t)], start=(kc == 0), stop=(kc == 7))
                for kc in range(8):
                    p.mm(bank(b0 + 1), wg[:, kc, :], XN[:, kc, sl(t)], start=(kc == 0), stop=(kc == 7))
                conv_chunk(bank(b0), xs_pp, t % 2, t == 0, 512, 40 + fc * 4, 72 + fc, 4, xc[:])
                p.copy(xcb[:], xc[:], eng="pool")
                p.mm(bank(b0 + 2), BDb[:, 0, fc, :], xcb[:])
                p.mm(bank(b0 + 3), BDb[:, 1, fc, :], xcb[:])
                p.act(rr[:], bank(b0 + 2), AF.Sigmoid, bias=PV[:, 80 + fc:81 + fc])
                p.act(ii[:], bank(b0 + 3), AF.Sigmoid, bias=PV[:, 88 + fc:89 + fc])
                p.act(aa[:], rr[:], AF.Exp, scale=CL[:, fc:fc + 1])
                p.tt(a2[:], aa[:], aa[:], ALU.mult)
                p.act(a2[:], a2[:], AF.Sqrt, bias=1.0, scale=-1.0)
                p.tt(ii[:], ii[:], xc[:], ALU.mult, eng="pool")
                p.tt(ii[:], ii[:], a2[:], ALU.mult)
                h_ = hs[t % 2]
                init = 0.0 if t == 0 else hs[1 - t % 2][:, 511:512]
                p.scan(h_[:], aa[:], ii[:], init, ALU.mult, ALU.add)
                gelu_mul(YA[:, fc, sl(t)], bank(b0 + 1), h_[:], g1[:], last_eng="pool")
        if debug == "ya" and l == 0:
            break
        ar.reset(mA)
        YA = ar.alloc("YA", [128, 8, S], BF16)
        proj_merge(l, YA, wba_d, 48, True)
        ar.reset(mA)
        if debug == "hA" and l == 0:
            break

        YBT = ar.alloc("YBT", [128, 8, S], BF16)
        mB = ar.mark()
        ekT = ar.alloc("ekT", [128, 16, 4], F32)
        thrT = ar.alloc("thrT", [128, 16, 4], F32)
        decB = ar.alloc("decB", [128, 4, 16], F32)
        mB2 = ar.mark()
        wifs = ar.alloc("wifs", [128, 64], F32)
        wifb = ar.alloc("wifb", [128, 8, 8], BF16)
        p.dma("sp", wifs[:], wif_d[l], "wif")
        p.copy(wifb[:], wifs[:].rearrange("p (k n) -> p k n", k=8))
        li = ar.alloc("li", [4, S], F32)
        xf = ar.alloc("xf", [4, S], F32)
        t1 = ar.alloc("t1", [4, S], F32)
        t2 = ar.alloc("t2", [4, S], F32)
        ones4 = ar.alloc("ones4", [4, S], F32)
        sm = ar.alloc("sm4", [4, 128], F32)
        smb = ar.alloc("sm4b", [4, 64], BF16)
        for t in range(4):
            for kc in range(8):
                p.mm(ps[0:4, t % 2, :], wifb[:, kc, 0:4], XN[:, kc, sl(t)], start=(kc == 0), stop=(kc == 7))
            for kc in range(8):
                p.mm(ps[0:4, 2 + t % 2, :], wifb[:, kc, 4:8], XN[:, kc, sl(t)], start=(kc == 0), stop=(kc == 7))
            p.act(li[:, sl(t)], ps[0:4, t % 2, :], AF.Identity, bias=PV[0:4, 376:377])
            p.act(xf[:, sl(t)], ps[0:4, 2 + t % 2, :], AF.Identity, bias=PV[0:4, 377:378])
        p.memset(ones4[:], 1.0)
        p.act(t1[:], xf[:], AF.Abs)
        p.act(t1[:], t1[:], AF.Exp, scale=-1.0)
        p.act(t1[:], t1[:], AF.Ln, bias=1.0)
        p.ts(t2[:], xf[:], 0.0, None, ALU.min)
        p.tt(t2[:], t2[:], t1[:], ALU.subtract)
        p.scan(xf[:], ones4[:], t2[:], 0.0, ALU.mult, ALU.add)
        p.tt(li[:], li[:], xf[:], ALU.subtract)
        cm = sm[:, 0:16]
        mt = sm[:, 16:32]
        mprev = sm[:, 32:48]
        dec = sm[:, 48:64]
        r1 = sm[:, 64:80]
        r2 = sm[:, 80:96]
        p.add("dve", lambda e, o=cm, i=li[:].rearrange("p (c n) -> p c n", c=16): e.tensor_reduce(o, i, AX.X, ALU.max),
              reads=[li[:]], writes=[cm])
        p.scan(mt, cm, cm, 0.0, ALU.max, ALU.max)
        p.memset(mprev[:, 0:1], 0.0)
        p.copy(mprev[:, 1:16], mt[:, 0:15])
        p.tt(dec, mprev, mt, ALU.subtract)
        p.act(dec, dec, AF.Exp)
        mtb = bc_last(mt, 128)
        p.tt(t1[:].rearrange("p (c n) -> p c n", c=16), li[:].rearrange("p (c n) -> p c n", c=16), mtb, ALU.subtract)
        p.act(t1[:], t1[:], AF.Exp)
        p.tt(t2[:].rearrange("p (c n) -> p c n", c=16), xf[:].rearrange("p (c n) -> p c n", c=16), mtb, ALU.add)
        p.act(t2[:], t2[:], AF.Exp, scale=-2.0, bias=2.0 * LN16 + float(np.log(EPS)))
        for c in range(16):
            p.tr(ps[:, 4, c * 4:(c + 1) * 4], t1[0:4, c * 128:(c + 1) * 128], CF[0:4, 0:4])
            p.tr(ps[:, 5, c * 4:(c + 1) * 4], t2[0:4, c * 128:(c + 1) * 128], CF[0:4, 0:4])
        p.copy(ekT[:].rearrange("p c h -> p (c h)"), ps[:, 4, 0:64])
        p.copy(thrT[:].rearrange("p c h -> p (c h)"), ps[:, 5, 0:64], eng="act")
        p.copy(smb[:, 0:16], dec)
        p.tt(r1, dec, smb[:, 0:16], ALU.subtract)
        p.copy(smb[:, 16:32], r1)
        p.tt(r2, r1, smb[:, 16:32], ALU.subtract)
        p.copy(smb[:, 32:48], r2)
        for h in range(4):
            for q3 in range(3):
                p.mm(ps[:, 6, h * 16:(h + 1) * 16], selb[0:4, h * 128:(h + 1) * 128], smb[0:4, q3 * 16:(q3 + 1) * 16],
                     start=(q3 == 0), stop=(q3 == 2))
        p.copy(decB[:].rearrange("p h c -> p (h c)"), ps[:, 6, 0:64])
        ar.reset(mB2)
        qT = ar.alloc("qT", [128, 2, S], BF16)
        kT = ar.alloc("kT", [128, 2, S], BF16)
        vh = ar.alloc("vh", [128, 16, 257], BF16)
        og = ar.alloc("og", [128, 16, 256], BF16)
        wpb = WPool(p, ar, "wb", 8, 128, 5)
        xs_pp = [ar.alloc(f"xsb{i}", [128, 515], F32) for i in range(2)]
        xcB2 = [ar.alloc(f"xcB{i}", [128, 512], F32) for i in range(2)]
        Sst = ar.alloc("Sst", [128, 2, 257], F32)
        Cb = ar.alloc("Cb", [128, 2, 257], BF16)
        wk = [ar.alloc(f"wk{i}", [128, 256], BF16) for i in range(2)]
        scm = [ar.alloc(f"scm{i}", [128, 128], BF16) for i in range(2)]
        hn = [ar.alloc(f"hn{i}", [128, 256], F32) for i in range(2)]
        ybk = [ar.alloc(f"ybk{i}", [128, 256], BF16) for i in range(2)]
        sml = [ar.alloc(f"sml{i}", [128, 16], F32) for i in range(2)]
        so2 = [ar.alloc(f"so{i}", [128, 256], F32) for i in range(2)]
        for h in range(4):
            for which in range(2):
                dst = qT if which == 0 else kT
                for dc in range(2):
                    ch = which * 8 + h * 2 + dc
                    w = wpb.load(win_d[l, 16 + ch])
                    for t in range(4):
                        b0 = t % 2
                        for kc in range(8):
                            p.mm(bank(b0), w[:, kc, :], XN[:, kc, sl(t)], start=(kc == 0), stop=(kc == 7))
                        conv_chunk(bank(b0), xs_pp, t % 2, t == 0, 512, 104 + ch * 4, 168 + ch, 4, xcB2[t % 2][:])
                        p.act(dst[:, dc, sl(t)], xcB2[t % 2][:], AF.Silu)
            wv = [wpb.load(win_d[l, 32 + h * 2 + e]) for e in range(2)]
            p.memset(vh[:, :, 256:257], 1.0)
            for c in range(16):
                b0 = 2 + c % 2
                for e in range(2):
                    for kc in range(8):
                        p.mm(ps[:, b0, e * 128:(e + 1) * 128], XN[:, kc, sl(c, 128)], wv[e][:, kc, :],
                             start=(kc == 0), stop=(kc == 7))
                p.copy(vh[:, c, 0:256], ps[:, b0, 0:256], eng="act")
            wo_ = [wpb.load(win_d[l, 40 + h * 2 + e]) for e in range(2)]
            for c in range(16):
                b0 = 2 + c % 2
                for e in range(2):
                    for kc in range(8):
                        p.mm(ps[:, b0, e * 128:(e + 1) * 128], XN[:, kc, sl(c, 128)], wo_[e][:, kc, :],
                             start=(kc == 0), stop=(kc == 7))
                p.act(so2[c % 2][:], ps[:, b0, 0:256], AF.Sigmoid)
                p.tt(og[:, c, :], so2[c % 2][:], GB[:, h * 256:(h + 1) * 256], ALU.mult)
            p.memset(Sst[:], 0.0)
            p.memset(Cb[:], 0.0)
            for c in range(16):
                cs = sl(c, 128)
                ek = ekT[:, c, h:h + 1]
                pb = c % 2
                ib = 5 + pb
                for dc in range(2):
                    p.mm(ps[:, 4, 128:256], kT[:, dc, cs], qT[:, dc, cs], start=(dc == 0), stop=(dc == 1))
                kb = bankb(4)
                for dc in range(2):
                    p.tr(kb[:, dc * 128:(dc + 1) * 128], kT[:, dc, cs], identb[:])
                p.ts(wk[pb][:], kb[:, 0:256], ek, None, ALU.mult)
                p.stt(scm[pb][:], ps[:, 4, 128:256], ek, maskf, ALU.mult, ALU.mult)
                for dc in range(2):
                    p.mm(ps[:, ib, 0:257], qT[:, dc, cs], Cb[:, dc, :], start=(dc == 0), stop=False)
                p.mm(ps[:, ib, 0:257], scm[pb][:], vh[:, c, :], start=False, stop=True)
                for dc in range(2):
                    p.mm(ps[:, 7, dc * 256:dc * 256 + 257] if False else ps[:, (0 if dc == 0 else 1), 0:257],
                         wk[pb][:, dc * 128:(dc + 1) * 128], vh[:, c, :])
                    p.stt(Sst[:, dc, :], Sst[:, dc, :], decB[:, h, c:c + 1], ps[:, (0 if dc == 0 else 1), 0:257],
                          ALU.mult, ALU.add)
                    if c < 15:
                        p.act(Cb[:, dc, :], Sst[:, dc, :], AF.Identity, scale=decB[:, h, c + 1:c + 2])
                s_ = sml[pb]
                p.act(s_[:, 14:15], ps[:, ib, 256:257], AF.Square, scale=float(np.sqrt(EPS)))
                p.add("dve", lambda e, o=s_[:, 2:8], i=ps[:, ib, 0:256]: e.bn_stats(o, i), reads=[ps[:, ib, 0:256]], writes=[s_[:, 2:8]])
                p.add("dve", lambda e, o=s_[:, 8:10], i=s_[:, 2:8]: e.bn_aggr(o, i), reads=[s_[:, 2:8]], writes=[s_[:, 8:10]])
                p.ts(s_[:, 10:11], s_[:, 14:15], thrT[:, c, h:h + 1], s_[:, 9:10], ALU.max, ALU.add)
                p.act(s_[:, 11:12], s_[:, 10:11], AF.Sqrt)
                p.add("dve", lambda e, o=s_[:, 13:14], i=s_[:, 11:12]: e.reciprocal(o, i), reads=[s_[:, 11:12]], writes=[s_[:, 13:14]])
                p.ts(hn[pb][:], ps[:, ib, 0:256], s_[:, 8:9], s_[:, 13:14], ALU.subtract, ALU.mult)
                p.tt(ybk[pb][:], hn[pb][:], og[:, c, :], ALU.mult)
                yb_ = bankb(7)
                for dc in range(2):
                    p.tr(yb_[:, dc * 128:(dc + 1) * 128], ybk[pb][:, dc * 128:(dc + 1) * 128], identb[:])
                p.copy(YBT[:, 2 * h:2 * h + 2, cs], yb_[:, 0:256].rearrange("p (k n) -> p k n", k=2), eng="act")
        ar.reset(mB)
        if debug == "ybt" and l == 0:
            break
        proj_merge(l, YBT, wbb_d, 56, False)
        ar.reset(mA)
        if debug == "hB" and l == 0:
            break

        rmsnorm(8)
        mX = ar.mark()
        kTm = ar.alloc("kTm", [128, 8, MEM], BF16)
        Vm = ar.alloc("Vm", [128, 2, D], BF16)
        QT = ar.alloc("QT", [128, 8, S], BF16)
        AOT = ar.alloc("AOT", [128, 8, S], BF16)
        wpx = WPool(p, ar, "wx", 8, 128, 4)
        pex = [ar.alloc(f"pex{i}", [128, 256], BF16) for i in range(2)]
        pT = [ar.alloc(f"pT{i}", [128, 2, 128], BF16) for i in range(2)]
        ao = [ar.alloc(f"ao{i}", [128, D], BF16) for i in range(2)]
        smx = [ar.alloc(f"smx{i}", [128, 8], F32) for i in range(2)]
        for j in range(8):
            w = wpx.load(wkv_d[l, j])
            for kc in range(8):
                p.mm(ps[:, j % 2, 0:256], w[:, kc, :], MT[:, kc, :], start=(kc == 0), stop=(kc == 7))
            p.copy(kTm[:, j, :], ps[:, j % 2, 0:256], eng="act")
        for j in range(8):
            w = wpx.load(wkv_d[l, 8 + j])
            for mc in range(2):
                for kc in range(8):
                    p.mm(ps[:, 2 + mc, 0:128], MT[:, kc, mc * 128:(mc + 1) * 128], w[:, kc, :], start=(kc == 0), stop=(kc == 7))
                p.copy(Vm[:, mc, j * 128:(j + 1) * 128], ps[:, 2 + mc, 0:128], eng="act" if mc else "dve")
        for j in range(8):
            w = wpx.load(wq_d[l, j])
            for t in range(4):
                b0 = t % 2
                for kc in range(8):
                    p.mm(bank(b0), w[:, kc, :], XN[:, kc, sl(t)], start=(kc == 0), stop=(kc == 7))
                p.copy(QT[:, j, sl(t)], bank(b0), eng="act" if t % 2 else "dve")
        for ti in range(16):
            cs = sl(ti, 128)
            ao_ = ao[ti % 2]
            for h in range(4):
                pb = h % 2
                sb_ = 2 + pb
                for dc in range(2):
                    p.mm(ps[:, sb_, 0:256], QT[:, 2 * h + dc, cs], kTm[:, 2 * h + dc, :], start=(dc == 0), stop=(dc == 1))
                s_ = smx[pb]
                p.add("dve", lambda e, o=s_[:, 0:1], i=ps[:, sb_, 0:256]: e.tensor_reduce(o, i, AX.X, ALU.max),
                      reads=[ps[:, sb_, 0:256]], writes=[s_[:, 0:1]])
                p.ts(s_[:, 1:2], s_[:, 0:1], -1.0 / 16.0, None, ALU.mult)
                p.act(pex[pb][:], ps[:, sb_, 0:256], AF.Exp, bias=s_[:, 1:2], scale=1.0 / 16.0)
                p.add("dve", lambda e, o=s_[:, 2:3], i=pex[pb][:]: e.tensor_reduce(o, i, AX.X, ALU.add),
                      reads=[pex[pb][:]], writes=[s_[:, 2:3]])
                p.add("dve", lambda e, o=s_[:, 3:4], i=s_[:, 2:3]: e.reciprocal(o, i), reads=[s_[:, 2:3]], writes=[s_[:, 3:4]])
                tb = bankb(4 + pb)
                for mc in range(2):
                    p.tr(tb[:, mc * 128:(mc + 1) * 128], pex[pb][:, mc * 128:(mc + 1) * 128], identb[:])
                p.copy(pT[pb][:], tb[:, 0:256].rearrange("p (k n) -> p k n", k=2), eng="act")
                for mc in range(2):
                    p.mm(ps[:, 6 + pb, 0:256], pT[pb][:, mc, :], Vm[:, mc, h * 256:(h + 1) * 256], start=(mc == 0), stop=(mc == 1))
                p.ts(ao_[:, h * 256:(h + 1) * 256], ps[:, 6 + pb, 0:256], s_[:, 3:4], None, ALU.mult)
            for b in range(2):
                tb = bankb(b)
                for k in range(4):
                    kc = b * 4 + k
                    p.tr(tb[:, k * 128:(k + 1) * 128], ao_[:, kc * 128:(kc + 1) * 128], identb[:])
                p.copy(AOT[:, b * 4:(b + 1) * 4, cs], tb[:, 0:512].rearrange("p (k n) -> p k n", k=4), eng="act" if b else "dve")
        out_proj(l, AOT, wo_d, wpx)
        ar.reset(mX)
        if debug == "hX" and l == 0:
            break

        rmsnorm(16)
        mF = ar.mark()
        ACTB = ar.alloc("ACTB", [128, 24, 1024], BF16)
        halo = ar.alloc("halo", [128, 48, 2], F32)
        wpf = WPool(p, ar, "wf", 8, 128, 4)
        xs_f = [ar.alloc(f"xsf{i}", [128, 1026], F32) for i in range(2)]
        accg = [ar.alloc(f"accg{i}", [128, 512], F32) for i in range(2)]
        accu = [ar.alloc(f"accu{i}", [128, 512], F32) for i in range(2)]
        f1 = [ar.alloc(f"f1{i}", [128, 512], F32) for i in range(2)]
        wdb = [ar.alloc(f"wdb{i}", [128, 24, 128], BF16) for i in range(2)]
        npar = 0
        for T in range(2):
            for j in range(24):
                for which in range(2):
                    cidx = which * 24 + j
                    w = wpf.load(wup_d[l, cidx])
                    wc = 184 + cidx * 3
                    xs = xs_f[npar % 2]
                    npar += 1
                    for hf in range(2):
                        b0 = which * 2 + hf + (j % 2) * 4
                        for kc in range(8):
                            p.mm(bank(b0), w[:, kc, :], XN[:, kc, T * 1024 + hf * 512:T * 1024 + (hf + 1) * 512],
                                 start=(kc == 0), stop=(kc == 7))
                        acc = (accg if which == 0 else accu)[hf]
                        o_ = hf * 512
                        p.act(acc[:], bank(b0), AF.Identity, bias=PV[:, 328 + cidx:329 + cidx], scale=PV[:, wc + 2:wc + 3])
                        p.copy(xs[:, 2 + o_:514 + o_], bank(b0), eng="act")
                        if hf == 0:
                            if T == 0:
                                p.memset(xs[:, 0:2], 0.0)
                            else:
                                p.copy(xs[:, 0:2], halo[:, cidx, :])
                        elif T == 0:
                            p.copy(halo[:, cidx, :], xs[:, 1024:1026])
                        p.stt(acc[:], xs[:, 1 + o_:513 + o_], PV[:, wc + 1:wc + 2], acc[:], ALU.mult, ALU.add)
                        p.stt(acc[:], xs[:, o_:512 + o_], PV[:, wc:wc + 1], acc[:], ALU.mult, ALU.add)
                for hf in range(2):
                    gelu_mul(ACTB[:, j, sl(hf)], accg[hf][:], accu[hf][:], f1[hf][:], last_eng=FFN_LAST, mid_eng=FFN_LAST)
            for jo in range(8):
                w = wdb[jo % 2]
                for part in range(3):
                    wpf.load(wdn_d[l, jo, :, part * 1024:(part + 1) * 1024], dst=w[:, part * 8:(part + 1) * 8, :],
                             key=f"wdb{jo % 2}_{part}")
                for hf in range(2):
                    b0 = hf
                    for kc in range(24):
                        p.mm(bank(b0), w[:, kc, :], ACTB[:, kc, sl(hf)], start=(kc == 0), stop=(kc == 23))
                    tsl = slice(T * 1024 + hf * 512, T * 1024 + (hf + 1) * 512)
                    p.tt(H[:, jo, tsl], H[:, jo, tsl], bank(b0), ALU.add)
        ar.reset(mF)

    if debug is None:
        rmsnorm_out = True
    mO = ar.mark()
    if debug in ("xn",):
        ar.reset(base)
        tmpf = ar.alloc("tmpf", [128, 8, 512], F32)
        for t in range(4):
            p.copy(tmpf[:], XN[:, :, sl(t)])
            for kc in range(8):
                p.dma("sp", dbg_d[:, kc * 2048 + t * 512:kc * 2048 + (t + 1) * 512], tmpf[:, kc, :], "dbg")
    elif debug in ("ya", "ybt"):
        ar.reset(mA)
        ar.alloc("keep", [128, 8, S], BF16)
        tmpf = ar.alloc("tmpf", [128, 8, 512], F32)
        src = YA if debug == "ya" else YBT
        for t in range(4):
            p.copy(tmpf[:], src[:, :, sl(t)])
            for kc in range(8):
                p.dma("sp", dbg_d[:, kc * 2048 + t * 512:kc * 2048 + (t + 1) * 512], tmpf[:, kc, :], "dbg")
    elif debug is not None:
        for kc in range(8):
            p.dma("sp", dbg_d[:, kc * 2048:(kc + 1) * 2048], H[:, kc, :], "dbg")
    if debug is not None:
        p.finish("sp", [dbg_d])
    ar.reset(base)
    sq = [ar.alloc(f"fsq{i}", [128, 512], BF16) for i in range(2)]
    rt = ar.alloc("frt", [128, 512], F32)
    rs = ar.alloc("frs", [128, 512], F32)
    on = ar.alloc("fon", [128, 8, 512], F32)
    ost = [ar.alloc(f"ost{i}", [128, D], F32) for i in range(2)]
    for t in range(4):
        for kc in range(8):
            s_ = sq[kc % 2]
            p.act(s_[:], H[:, kc, sl(t)], AF.Square)
            p.mm(bank(t), onesb[:], s_[:], start=(kc == 0), stop=(kc == 7))
        p.act(rt[:], bank(t), AF.Sqrt, bias=EPS, scale=1.0 / D)
        p.add("dve", lambda e, o=rs[:], i=rt[:]: e.reciprocal(o, i), reads=[rt[:]], writes=[rs[:]])
        for kc in range(8):
            p.stt(on[:, kc, :], H[:, kc, sl(t)], PV[:, 24 + kc:25 + kc], rs[:], ALU.mult, ALU.mult)
        for q4 in range(4):
            ti = t * 4 + q4
            o_ = ost[ti % 2]
            for b in range(2):
                for k in range(4):
                    kc = b * 4 + k
                    p.tr(ps[:, 4 + b, k * 128:(k + 1) * 128], on[:, kc, q4 * 128:(q4 + 1) * 128], identf)
                p.copy(o_[:, b * 512:(b + 1) * 512], ps[:, 4 + b, :], eng="act" if b else "dve")
            p.dma("sp", out_d[ti * 128:(ti + 1) * 128, :], o_[:], f"ost{ti % 2}")
    p.finish("sp", [out_d])
    cnt = p.emit()
    return nc, cnt, len(p.ops)


def _tiles(W, ncols=128):
    K, N = W.shape
    kc = K // 128
    nt = N // ncols
    return np.ascontiguousarray(
        W.reshape(kc, 128, nt, ncols).transpose(2, 1, 0, 3)).reshape(nt, 128, kc * ncols)


def _fm(v):
    return np.ascontiguousarray(v.reshape(-1, 128).T)


def prep_inputs(inp):
    f = lambda a: np.asarray(a, dtype=np.float32)
    consts = np.zeros((128, 768), np.float32)
    consts[:, 0:128] = np.eye(128, dtype=np.float32)
    consts[:, 128:256] = np.triu(np.ones((128, 128), np.float32))
    for h in range(4):
        consts[h, 256 + h * 128:256 + (h + 1) * 128] = 1.0
    pvec = np.zeros((L, 128, NP), np.float32)
    gbc = np.zeros((L, 128, D), np.float32)
    bd = np.zeros((L, 2, 128, 8, 128), np.float32)
    sh = {}
    w_in = f(inp["w_in"])
    for l in range(L):
        pv = pvec[l]
        pv[:, 0:8] = _fm(f(inp["norm_mix_g"])[l])
        pv[:, 8:16] = _fm(f(inp["norm_xa_g"])[l])
        pv[:, 16:24] = _fm(f(inp["norm_ffn_g"])[l])
        pv[:, 24:32] = _fm(f(inp["final_norm_g"]))
        pv[:, 32:40] = _fm(f(inp["mem_norm_g"]))
        cw = f(inp["rnn_conv_w"])[l]
        pv[:, 40:72] = np.stack([_fm(cw[j]) for j in range(4)], axis=2).reshape(128, 32)
        pv[:, 72:80] = _fm(f(inp["rnn_conv_b"])[l])
        pv[:, 80:88] = _fm(f(inp["lru_ba"])[l])
        pv[:, 88:96] = _fm(f(inp["lru_bx"])[l])
        pv[:, 96:104] = _fm(f(inp["lru_lambda"])[l])
        mw = f(inp["ml_conv_w"])[l]
        pv[:, 104:168] = np.stack([_fm(mw[j]) for j in range(4)], axis=2).reshape(128, 64)
        pv[:, 168:184] = _fm(f(inp["ml_conv_b"])[l])
        fw_ = f(inp["ffn_conv_w"])[l]
        pv[:, 184:328] = np.stack([_fm(fw_[j]) for j in range(3)], axis=2).reshape(128, 144)
        pv[:, 328:376] = _fm(f(inp["ffn_conv_b"])[l])
        ifb = f(inp["ml_if_b"])[l]
        pv[0:4, 376] = ifb[0:4]
        pv[0:4, 377] = ifb[4:8]
        gbc[l] = np.broadcast_to(f(inp["ml_norm_g"])[l][None, :], (128, D))
        for gi, nm in enumerate(("lru_wa", "lru_wx")):
            wg = f(inp[nm])[l]
            for g in range(16):
                fc, hb = g // 2, g % 2
                bd[l, gi, hb * 64:(hb + 1) * 64, fc, hb * 64:(hb + 1) * 64] = wg[g]
    main = np.concatenate([w_in[:, :, 0:6144], w_in[:, :, 6152:8200]], axis=2)
    sh["w_in_t"] = np.stack([_tiles(main[l]) for l in range(L)])
    wif = w_in[:, :, 6144:6152]
    sh["w_if"] = np.stack([np.ascontiguousarray(wif[l].reshape(8, 128, 8).transpose(1, 0, 2)).reshape(128, 64) for l in range(L)])
    for k_, nm in (("w_ba", "w_branch_a"), ("w_bb", "w_branch_b"), ("w_mo", "w_mix_out"), ("xa_wq", "xa_wq"),
                   ("xa_wkv", "xa_wkv"), ("xa_wo", "xa_wo"), ("ffn_up", "ffn_w_up"), ("ffn_dn", "ffn_w_down")):
        a = f(inp[nm])
        sh[k_] = np.stack([_tiles(a[l]) for l in range(L)])
    sh["consts"] = consts
    sh["pvec"] = pvec
    sh["gbc"] = gbc
    sh["bd"] = bd.reshape(L, 2, 128, 1024)
    return sh


_CACHE = {}


def kernel(**inputs):
    sh = prep_inputs(inputs)
    x = np.asarray(inputs["x"], dtype=np.float32)
    mem = np.asarray(inputs["mem"], dtype=np.float32)
    if "nc" not in _CACHE:
        _CACHE["nc"] = build(None)[0]
    nc = _CACHE["nc"]
    in_maps = []
    for b in range(8):
        m = dict(sh)
        m["x"] = np.ascontiguousarray(x[b])
        m["mem"] = np.ascontiguousarray(mem[b])
        in_maps.append(m)
    res = run_bass_kernel_spmd(nc, in_maps, core_ids=list(range(8)))
    return np.stack([np.asarray(res.results[b]["out"], dtype=np.float32) for b in range(8)])
```

```python
import contextlib
import numpy as np
import concourse.bass as bass
import concourse.mybir as mybir
from concourse.bass_utils import run_bass_kernel_spmd

F32 = mybir.dt.float32
BF16 = mybir.dt.bfloat16
AF = mybir.ActivationFunctionType
ALU = mybir.AluOpType
AX = mybir.AxisListType


def _esize(dt):
    return mybir.dt.size(dt)


class Op:
    __slots__ = ("eng", "fn", "deps", "idx", "signal", "val", "is_dma", "dkey", "nm", "cost", "alldeps", "fcls")

    def __init__(self, eng, fn, idx, is_dma=False, dkey=None, nm=""):
        self.eng = eng
        self.fn = fn
        self.idx = idx
        self.deps = {}
        self.signal = False
        self.val = 0
        self.is_dma = is_dma
        self.dkey = dkey
        self.nm = nm
        self.cost = 100.0
        self.alldeps = None
        self.fcls = None


class Prog:
    ENGS = ("pe", "act", "dve", "pool", "sp")

    def __init__(self, nc):
        self.nc = nc
        self.ops = []
        self.recs = {}
        self.dma_cnt = {}
        self.sb_lo = 16512
        self.sb_hi = 229344
        self.names = 0

    def sbuf(self, name, shape, dtype, offset):
        self.names += 1
        nbytes = int(np.prod(shape[1:])) * _esize(dtype)
        assert offset % 32 == 0, (name, offset)
        assert self.sb_lo + offset + nbytes <= self.sb_hi, (name, offset, nbytes)
        return self.nc.alloc_sbuf_tensor_at(
            f"{name}_{self.names}", list(shape), dtype, offset=self.sb_lo + offset)

    def _foot(self, ap):
        t = ap.tensor
        es = _esize(ap.dtype)
        dims = ap.ap
        space = str(ap.space)
        off = ap.offset
        if "SB" in space or "PSUM" in space:
            pstep, pcnt = dims[0]
            if pstep > 0:
                p0 = off // pstep
                foff = off % pstep
            else:
                p0 = 0
                foff = off
            if "SB" in space:
                base = t.manual_sbuf_range[0]
                key = "SB"
            else:
                base = 0
                key = "PS"
            lo = base + foff * es
            if key == "PS":
                hi = lo
                for s_, c_ in dims[1:]:
                    hi += (c_ - 1) * abs(s_) * es
                b_lo = lo // 2048
                b_hi = hi // 2048
                return [("PS", 0, 128, b * 2048, (b + 1) * 2048) for b in range(b_lo, b_hi + 1)]
            free = [d for d in dims[1:] if d[1] > 1]
            if not free:
                return [(key, p0, p0 + pcnt, lo, lo + es)]
            inner = free[-1]
            ilen = ((inner[1] - 1) * abs(inner[0]) + 1) * es
            outer = free[:-1]
            nout = 1
            for s, c in outer:
                nout *= c
            if nout > 64 or any(s < 0 for s, c in free):
                hi = lo
                for s, c in free:
                    hi += (c - 1) * abs(s) * es
                return [(key, p0, p0 + pcnt, lo, hi + es)]
            res = []
            idxs = [0] * len(outer)
            while True:
                o = lo
                for (s, c), i in zip(outer, idxs):
                    o += i * s * es
                res.append((key, p0, p0 + pcnt, o, o + ilen))
                k = len(outer) - 1
                while k >= 0:
                    idxs[k] += 1
                    if idxs[k] < outer[k][1]:
                        break
                    idxs[k] = 0
                    k -= 1
                if k < 0:
                    break
            return res
        else:
            lo = off * es
            hi = lo
            for s, c in dims:
                hi += (c - 1) * abs(s) * es
            return [("D:" + t.name, 0, 1, lo, hi + es)]

    def add(self, eng, fn, reads=(), writes=(), dkey=None, nm=""):
        idx = len(self.ops)
        is_dma = dkey is not None
        op = Op(eng, fn, idx, is_dma, dkey, nm)
        ekey = ("dma", idx) if is_dma else eng
        newrecs = []
        for ap in reads:
            for (key, p0, p1, lo, hi) in self._foot(ap):
                d = self.recs.setdefault(key, {})
                if key == "PS":
                    dead = []
                    for rk, oi in d.items():
                        (q0, q1, l2, h2, ek, isw, tw) = rk
                        if p0 < q1 and q0 < p1 and lo < h2 and l2 < hi:
                            if tw:
                                op.deps[oi] = "raw"
                            elif oi not in op.deps:
                                op.deps[oi] = "war"
                            dead.append(rk)
                    for rk in dead:
                        del d[rk]
                    newrecs.append((key, (p0, p1, lo, hi, ekey, True, False)))
                    continue
                for (q0, q1, l2, h2, ek, isw, tw), oi in d.items():
                    if isw and p0 < q1 and q0 < p1 and lo < h2 and l2 < hi:
                        op.deps[oi] = "raw"
                newrecs.append((key, (p0, p1, lo, hi, ekey, False, False)))
        for ap in writes:
            for (key, p0, p1, lo, hi) in self._foot(ap):
                d = self.recs.setdefault(key, {})
                dead = []
                for rk, oi in d.items():
                    (q0, q1, l2, h2, ek, isw, tw) = rk
                    if p0 < q1 and q0 < p1 and lo < h2 and l2 < hi:
                        if oi not in op.deps:
                            op.deps[oi] = "waw" if tw else "war"
                        if p0 <= q0 and q1 <= p1 and lo <= l2 and h2 <= hi:
                            dead.append(rk)
                for rk in dead:
                    del d[rk]
                newrecs.append((key, (p0, p1, lo, hi, ekey, True, True)))
        for key, rk in newrecs:
            prev = self.recs[key].get(rk)
            if prev is not None and prev != idx and prev not in op.deps:
                op.deps[prev] = "ord"
            self.recs[key][rk] = idx
        if is_dma:
            self.dma_cnt[dkey] = self.dma_cnt.get(dkey, 0) + 16
            op.val = self.dma_cnt[dkey]
        try:
            if eng == "pe":
                n = reads[1].free_size() if len(reads) > 1 else 128
                op.cost = max(64.0, float(n)) / 2.0 + 12.0
            elif is_dma:
                a = writes[0]
                op.cost = 2000.0 + a.partition_size() * a.free_size() * _esize(a.dtype) / 150.0
            elif writes:
                a = writes[0]
                op.cost = 120.0 + a.free_size() / 0.96
                if reads and "PSUM" in str(reads[0].space):
                    op.cost += 60.0
                if nm == "scan":
                    op.cost += a.free_size() / 0.96
        except Exception:
            pass
        self.ops.append(op)
        return op

    def mm(self, out, lhsT, rhs, start=True, stop=True):
        return self.add("pe", lambda e: e.matmul(out, lhsT, rhs, start=start, stop=stop),
                        reads=[lhsT, rhs], writes=[out])

    def tr(self, out, in_, ident):
        return self.add("pe", lambda e: e.transpose(out, in_, ident),
                        reads=[in_, ident], writes=[out])

    def act(self, out, in_, func, bias=None, scale=None, eng="act"):
        kw = {}
        rd = [in_]
        if bias is not None:
            kw["bias"] = bias
            if not isinstance(bias, (int, float)):
                rd.append(bias)
        if scale is not None:
            kw["scale"] = scale
            if not isinstance(scale, (int, float)):
                rd.append(scale)
        op = self.add(eng, lambda e: e.activation(out, in_, func, **kw), reads=rd, writes=[out])
        if func in (AF.Sigmoid, AF.Silu, AF.Sqrt):
            op.fcls = str(func)
        elif func in (AF.Exp, AF.Ln):
            op.fcls = "explog"
        return op

    def ts(self, out, in0, s1, s2, op0, op1=None, eng="dve"):
        rd = [in0]
        for s in (s1, s2):
            if s is not None and not isinstance(s, (int, float)):
                rd.append(s)
        if op1 is None:
            return self.add(eng, lambda e: e.tensor_scalar(out, in0, s1, None, op0), reads=rd, writes=[out])
        return self.add(eng, lambda e: e.tensor_scalar(out, in0, s1, s2, op0, op1), reads=rd, writes=[out])

    def tt(self, out, in0, in1, op, eng="dve"):
        return self.add(eng, lambda e: e.tensor_tensor(out, in0, in1, op), reads=[in0, in1], writes=[out])

    def stt(self, out, in0, scalar, in1, op0, op1, eng="dve"):
        rd = [in0, in1]
        if not isinstance(scalar, (int, float)):
            rd.append(scalar)
        return self.add(eng, lambda e: e.scalar_tensor_tensor(out, in0, scalar, in1, op0, op1),
                        reads=rd, writes=[out])

    def copy(self, out, in_, eng="dve"):
        if eng == "act":
            return self.act(out, in_, AF.Copy)
        return self.add(eng, lambda e: e.tensor_copy(out, in_), reads=[in_], writes=[out])

    def memset(self, out, val, eng="dve"):
        return self.add(eng, lambda e: e.memset(out, val), reads=[], writes=[out])

    def scan(self, out, d0, d1, initial, op0, op1):
        rd = [d0, d1]
        if not isinstance(initial, (int, float)):
            rd.append(initial)
        return self.add("dve", lambda e: e.tensor_tensor_scan(out, d0, d1, initial, op0, op1),
                        reads=rd, writes=[out], nm="scan")

    def dma(self, q, out, in_, dkey):
        return self.add(q, lambda e: e.dma_start(out=out, in_=in_), reads=[in_], writes=[out], dkey=dkey)

    def finish(self, eng, aps):
        return self.add(eng, None, reads=list(aps), writes=[])

    def schedule(self):
        import heapq
        ops = self.ops
        n = len(ops)
        succ = [[] for _ in range(n)]
        indeg = [0] * n
        for op in ops:
            op.alldeps = dict(op.deps)
            for d in op.deps:
                succ[d].append(op.idx)
                indeg[op.idx] += 1
        fin = [0.0] * n
        ready = [0.0] * n
        tail = [0.0] * n
        for op in reversed(ops):
            m_ = 0.0
            for sidx in succ[op.idx]:
                if tail[sidx] > m_:
                    m_ = tail[sidx]
            tail[op.idx] = op.cost + 300.0 + m_
        released = {e: [] for e in self.ENGS}
        for op in ops:
            if indeg[op.idx] == 0:
                released[op.eng].append(op.idx)
        tnow = {e: 0.0 for e in self.ENGS}
        order = {e: [] for e in self.ENGS}
        last_tbl = [None]
        remaining = n
        active = set(self.ENGS)
        while remaining > 0:
            cand = [e for e in self.ENGS if released[e]]
            assert cand, "scheduler deadlock"
            e = min(cand, key=lambda k: tnow[k])
            t = tnow[e]
            rl = released[e]
            best = None
            bk = None
            for i in rl:
                pen = 0.0
                if e == "act":
                    fc_ = ops[i].fcls
                    if fc_ is not None and fc_ != last_tbl[0]:
                        pen = 1300.0
                k = (int((max(ready[i], t) + pen) // 250.0), -tail[i], i)
                if bk is None or k < bk:
                    bk = k
                    best = i
            rl.remove(best)
            op = ops[best]
            st_ = max(ready[best], t)
            if e == "act" and op.fcls is not None:
                if op.fcls != last_tbl[0]:
                    st_ += 1300.0
                last_tbl[0] = op.fcls
            if op.is_dma:
                issue = 1500.0 if e == "pool" else 100.0
                tnow[e] = st_ + issue
                fin[best] = st_ + op.cost
            elif op.fn is None:
                tnow[e] = st_
                fin[best] = st_
            else:
                tnow[e] = st_ + op.cost
                fin[best] = st_ + op.cost + (60.0 if e == "pe" else 350.0)
            order[e].append(op)
            remaining -= 1
            for sidx in succ[best]:
                indeg[sidx] -= 1
                if fin[best] > ready[sidx]:
                    ready[sidx] = fin[best]
                if indeg[sidx] == 0:
                    released[ops[sidx].eng].append(sidx)
        self.est_ns = max(tnow.values())
        return order

    def emit(self, sched=True):
        nc = self.nc
        ops = self.ops
        if sched:
            per = self.schedule()
        else:
            per = {e: [op for op in ops if op.eng == e] for e in self.ENGS}
        for op in ops:
            need = {}
            for di, kind in op.deps.items():
                p = ops[di]
                if p.is_dma:
                    need[di] = kind
                elif p.eng == op.eng and not op.is_dma:
                    if op.eng == "pe":
                        continue
                    if kind in ("raw", "war", "waw"):
                        need[di] = kind
                else:
                    need[di] = kind
            op.deps = need
            for di in need:
                if not ops[di].is_dma:
                    ops[di].signal = True
        cnt = {e: 0 for e in self.ENGS}
        for e in self.ENGS:
            for op in per[e]:
                if not op.is_dma and op.signal:
                    cnt[e] += 1
                    op.val = cnt[e]
        with contextlib.ExitStack() as st:
            sems = {}
            for e in self.ENGS:
                sems[e] = st.enter_context(nc.semaphore("s_" + e))
            for k in self.dma_cnt:
                sems[("d", k)] = st.enter_context(nc.semaphore("d_" + str(k)))
            def run(ename, eh):
                waited = {}
                for op in per[ename]:
                    w = {}
                    for di in op.deps:
                        p = ops[di]
                        sk = ("d", p.dkey) if p.is_dma else p.eng
                        if p.val > w.get(sk, 0):
                            w[sk] = p.val
                    for sk, v in w.items():
                        if v > waited.get(sk, 0):
                            eh.wait_ge(sems[sk], v)
                            waited[sk] = v
                    if op.fn is None:
                        continue
                    ins = op.fn(eh)
                    if op.is_dma:
                        ins.then_inc(sems[("d", op.dkey)], 16)
                    elif op.signal:
                        ins.then_inc(sems[ename], 1)

            with nc.Block() as block:
                @block.tensor
                def _(e):
                    run("pe", e)

                @block.scalar
                def _(e):
                    run("act", e)

                @block.vector
                def _(e):
                    run("dve", e)

                @block.gpsimd
                def _(e):
                    run("pool", e)

                @block.sync
                def _(e):
                    run("sp", e)
        return cnt


S = 2048
D = 1024
L = 2
MEM = 256
NP = 384
EPS = 1e-6
LN16 = float(np.log(16.0))
FFN_LAST = "pool"
GC = 1.5957691216057308


def bc_last(ap, n):
    return bass.AP(ap.tensor, ap.offset, [list(d) for d in ap.ap] + [[0, n]])


class Arena:
    def __init__(self, p):
        self.p = p
        self.off = 0

    def alloc(self, name, shape, dtype):
        nbytes = int(np.prod(shape[1:])) * _esize(dtype)
        o = self.off
        self.off = (o + nbytes + 31) // 32 * 32
        return self.p.sbuf(name, shape, dtype, o)

    def mark(self):
        return self.off

    def reset(self, m):
        self.off = m


class WPool:
    def __init__(self, p, ar, name, kc, ncols, nslots=4):
        self.p = p
        self.kc = kc
        self.ncols = ncols
        self.bf = [ar.alloc(f"{name}_bf{i}", [128, kc, ncols], BF16) for i in range(nslots)]
        self.name = name
        self.n = 0

    def load(self, src_tile, dst=None, key=None):
        p = self.p
        if dst is None:
            i = self.n
            self.n += 1
            dst = self.bf[i % len(self.bf)][:]
            key = f"{self.name}{i % len(self.bf)}"
        kc = dst.shape[1]
        p.dma("pool", dst, src_tile.rearrange("p (k n) -> p k n", k=kc), key)
        return dst


def build(debug=None):
    nc = bass.Bass("TRN2", target_bir_lowering=False)
    dt_in = lambda n, s: nc.dram_tensor(n, list(s), F32, kind="ExternalInput").ap()
    x_d = dt_in("x", [S, D])
    mem_d = dt_in("mem", [MEM, D])
    const_d = dt_in("consts", [128, 768])
    pvec_d = dt_in("pvec", [L, 128, NP])
    gb_d = dt_in("gbc", [L, 128, D])
    bd_d = dt_in("bd", [L, 2, 128, 8 * 128])
    win_d = dt_in("w_in_t", [L, 64, 128, 8 * 128])
    wif_d = dt_in("w_if", [L, 128, 8 * 8])
    wba_d = dt_in("w_ba", [L, 8, 128, 8 * 128])
    wbb_d = dt_in("w_bb", [L, 8, 128, 8 * 128])
    wmo_d = dt_in("w_mo", [L, 8, 128, 8 * 128])
    wq_d = dt_in("xa_wq", [L, 8, 128, 8 * 128])
    wkv_d = dt_in("xa_wkv", [L, 16, 128, 8 * 128])
    wo_d = dt_in("xa_wo", [L, 8, 128, 8 * 128])
    wup_d = dt_in("ffn_up", [L, 48, 128, 8 * 128])
    wdn_d = dt_in("ffn_dn", [L, 8, 128, 24 * 128])
    out_d = nc.dram_tensor("out", [S, D], F32, kind="ExternalOutput").ap()
    y1_d = nc.dram_tensor("y1s", [8, 128, S], BF16, kind="Internal").ap()
    dbg_d = None
    if debug is not None:
        dbg_d = nc.dram_tensor("dbg", [128, 8 * 2048], F32, kind="ExternalOutput").ap()

    p = Prog(nc)
    ar = Arena(p)
    ps = nc.alloc_psum_tensor("ps", [128, 8, 512], F32)

    def bank(i):
        return ps[:, i % 8, :]

    def bankb(i):
        return ps[:, i % 8, :].bitcast(BF16)

    H = ar.alloc("H", [128, 8, S], F32)
    XN = ar.alloc("XN", [128, 8, S], BF16)
    CF = ar.alloc("CF", [128, 768], F32)
    identb = ar.alloc("identb", [128, 128], BF16)
    onesb = ar.alloc("onesb", [128, 128], BF16)
    selb = ar.alloc("selb", [4, 512], BF16)
    PV = ar.alloc("PV", [128, NP], F32)
    GB = ar.alloc("GB", [128, D], F32)
    BDb = ar.alloc("BDb", [128, 2, 8, 128], BF16)
    MT = ar.alloc("memnT", [128, 8, MEM], BF16)
    CL = ar.alloc("CL", [128, 16], F32)
    identf = CF[:, 0:128]
    maskf = CF[:, 128:256]
    base = ar.mark()

    def sl(t, n=512):
        return slice(t * n, (t + 1) * n)

    p.dma("sp", CF[:], const_d, "c0")
    p.copy(identb[:], CF[:, 0:128])
    p.memset(onesb[:], 1.0)
    p.copy(selb[:], CF[0:4, 256:768])

    m0 = ar.mark()
    xst = [ar.alloc(f"xst{i}", [128, D], F32) for i in range(2)]
    for t in range(16):
        st = xst[t % 2]
        p.dma("sp", st[:], x_d[t * 128:(t + 1) * 128, :], f"xst{t % 2}")
        for b in range(2):
            for k in range(4):
                kc = b * 4 + k
                p.tr(ps[:, (t * 2 + b) % 8, k * 128:(k + 1) * 128], st[:, kc * 128:(kc + 1) * 128], identf)
            src = ps[:, (t * 2 + b) % 8, :].rearrange("p (k n) -> p k n", k=4)
            p.copy(H[:, b * 4:(b + 1) * 4, t * 128:(t + 1) * 128], src, eng="act" if b else "dve")

    def load_params(l):
        p.dma("sp", PV[:], pvec_d[l], "pv")
        p.dma("sp", GB[:], gb_d[l], "gb")
        bst = xst[0]
        for g in range(2):
            p.dma("sp", bst[:], bd_d[l, g], "bdst")
            p.copy(BDb[:, g, :, :], bst[:].rearrange("p (k n) -> p k n", k=8))
        tmp = xst[1]
        p.act(tmp[:, 0:8], PV[:, 96:104], AF.Exp, scale=-1.0)
        p.act(tmp[:, 8:16], tmp[:, 0:8], AF.Ln, bias=1.0)
        p.ts(CL[:, 0:8], tmp[:, 8:16], -8.0, None, ALU.mult)
        p.ts(CL[:, 8:16], tmp[:, 8:16], -16.0, None, ALU.mult)

    load_params(0)
    mst = xst
    scr = ar.alloc("mscr", [128, D], F32)
    small = ar.alloc("msm", [128, 8], F32)
    for mc in range(2):
        st = mst[mc]
        p.dma("sp", st[:], mem_d[mc * 128:(mc + 1) * 128, :], f"xst{mc}")
        p.tt(scr[:], st[:], st[:], ALU.mult)
        p.add("dve", lambda e, o=small[:, 0:1], i=scr[:]: e.tensor_reduce(o, i, AX.X, ALU.add),
              reads=[scr[:]], writes=[small[:, 0:1]])
        p.act(small[:, 1:2], small[:, 0:1], AF.Sqrt, bias=EPS, scale=1.0 / D)
        p.add("dve", lambda e, o=small[:, 2:3], i=small[:, 1:2]: e.reciprocal(o, i),
              reads=[small[:, 1:2]], writes=[small[:, 2:3]])
        p.ts(scr[:], st[:], small[:, 2:3], None, ALU.mult)
        for b in range(2):
            for k in range(4):
                kc = b * 4 + k
                p.tr(ps[:, b, k * 128:(k + 1) * 128], scr[:, kc * 128:(kc + 1) * 128], identf)
            for k in range(4):
                kc = b * 4 + k
                p.ts(MT[:, kc, mc * 128:(mc + 1) * 128], ps[:, b, k * 128:(k + 1) * 128],
                     PV[:, 32 + kc:33 + kc], None, ALU.mult)
    ar.reset(m0)

    def rmsnorm(gcol0):
        m = ar.mark()
        sq = [ar.alloc(f"sq{i}", [128, 512], BF16) for i in range(2)]
        rt = ar.alloc("rt", [128, 512], F32)
        rs = ar.alloc("rs", [128, 512], F32)
        for t in range(4):
            for kc in range(8):
                s_ = sq[kc % 2]
                p.act(s_[:], H[:, kc, sl(t)], AF.Square)
                p.mm(bank(t), onesb[:], s_[:], start=(kc == 0), stop=(kc == 7))
            p.act(rt[:], bank(t), AF.Sqrt, bias=EPS, scale=1.0 / D)
            p.add("dve", lambda e, o=rs[:], i=rt[:]: e.reciprocal(o, i), reads=[rt[:]], writes=[rs[:]])
            for kc in range(8):
                p.stt(XN[:, kc, sl(t)], H[:, kc, sl(t)], PV[:, gcol0 + kc:gcol0 + kc + 1], rs[:],
                      ALU.mult, ALU.mult)
        ar.reset(m)

    def gelu_mul(out, xin, other, tmp, last_eng="dve", mid_eng="dve"):
        t1 = tmp
        p.act(t1, xin, AF.Square, scale=float(np.sqrt(0.044715)))
        p.stt(t1, t1, 1.0, xin, ALU.add, ALU.mult)
        p.act(t1, t1, AF.Sigmoid, scale=GC)
        p.tt(t1, t1, xin, ALU.mult, eng=mid_eng)
        p.tt(out, t1, other, ALU.mult, eng=last_eng)

    def conv_chunk(psb, xs_pp, par, first, nt, wcol0, bcol, ntap, acc):
        hl = ntap - 1
        xs = xs_pp[par]
        p.copy(xs[:, hl:hl + nt], psb, eng="act")
        if first:
            p.memset(xs[:, 0:hl], 0.0)
        else:
            p.copy(xs[:, 0:hl], xs_pp[1 - par][:, nt:nt + hl])
        p.act(acc, psb, AF.Identity, bias=PV[:, bcol:bcol + 1], scale=PV[:, wcol0 + hl:wcol0 + hl + 1])
        for j in range(hl):
            p.stt(acc, xs[:, j:j + nt], PV[:, wcol0 + j:wcol0 + j + 1], acc, ALU.mult, ALU.add)

    def proj_merge(l, src, wb_d, gtile0, first):
        m = ar.mark()
        Y = ar.alloc("Ymerge", [128, 8, S], BF16)
        wp = WPool(p, ar, "wm", 8, 128, 4)
        sg = [ar.alloc(f"sg{i}", [128, 512], F32) for i in range(2)]
        y1b = None
        if not first:
            y1b = [ar.alloc(f"y1b{i}", [128, S], BF16) for i in range(2)]
        for j in range(8):
            wb = wp.load(wb_d[l, j])
            wg = wp.load(win_d[l, gtile0 + j])
            if not first:
                p.dma("sp", y1b[j % 2][:], y1_d[j], f"y1b{j % 2}")
            for t in range(4):
                b0 = (t % 2) * 2
                for kc in range(8):
                    p.mm(bank(b0), wb[:, kc, :], src[:, kc, sl(t)], start=(kc == 0), stop=(kc == 7))
                for kc in range(8):
                    p.mm(bank(b0 + 1), wg[:, kc, :], XN[:, kc, sl(t)], start=(kc == 0), stop=(kc == 7))
                s_ = sg[t % 2]
                p.act(s_[:], bank(b0 + 1), AF.Sigmoid)
                if first:
                    p.tt(Y[:, j, sl(t)], bank(b0), s_[:], ALU.mult)
                else:
                    p.tt(s_[:], bank(b0), s_[:], ALU.mult)
                    p.tt(Y[:, j, sl(t)], s_[:], y1b[j % 2][:, sl(t)], ALU.add)
            if first:
                p.dma("sp", y1_d[j], Y[:, j, :], f"y1w{j}")
        if not first:
            for j in range(8):
                wm = wp.load(wmo_d[l, j])
                for t in range(4):
                    b0 = 4 + (t % 2)
                    for kc in range(8):
                        p.mm(bank(b0), wm[:, kc, :], Y[:, kc, sl(t)], start=(kc == 0), stop=(kc == 7))
                    p.tt(H[:, j, sl(t)], H[:, j, sl(t)], bank(b0), ALU.add)
        ar.reset(m)

    def out_proj(l, src, w_d, wp):
        for j in range(8):
            wm = wp.load(w_d[l, j])
            for t in range(4):
                b0 = 4 + (t % 2)
                for kc in range(8):
                    p.mm(bank(b0), wm[:, kc, :], src[:, kc, sl(t)], start=(kc == 0), stop=(kc == 7))
                p.tt(H[:, j, sl(t)], H[:, j, sl(t)], bank(b0), ALU.add)

    def dump(ap3):
        shp = ap3.shape
        n = int(np.prod(shp[1:]))
        m = ar.mark()
        p.dma("sp", dbg_d[:, 0:n], ap3, "dbg") if ap3.dtype == F32 else None
        ar.reset(m)

    for l in range(L):
        if l > 0:
            load_params(l)
        rmsnorm(0)
        if debug == "xn" and l == 0:
            break
        mA = ar.mark()
        YA = ar.alloc("YA", [128, 8, S], BF16)
        mA1 = ar.mark()
        wp = WPool(p, ar, "wa", 8, 128, 4)
        xs_pp = [ar.alloc(f"xs{i}", [128, 515], F32) for i in range(2)]
        def dbl(nm, dt_=F32):
            return [ar.alloc(f"{nm}{i}", [128, 512], dt_) for i in range(2)]
        xc2, xcb2, rr2, ii2, aa2, a22, xg2, g12 = dbl("xc"), dbl("xcb", BF16), dbl("rr"), dbl("ii"), dbl("aa"), dbl("a2"), dbl("xg"), dbl("g1")
        hs = [ar.alloc(f"hs{i}", [128, 512], F32) for i in range(2)]
        for fc in range(8):
            wx = wp.load(win_d[l, fc])
            wg = wp.load(win_d[l, 8 + fc])
            for t in range(4):
                b0 = (t % 2) * 4
                xc, xcb, rr, ii, aa, a2, xg, g1 = (v[t % 2] for v in (xc2, xcb2, rr2, ii2, aa2, a22, xg2, g12))
                for kc in range(8):
                    p.mm(bank(b0), wx[:, kc, :], XN[:, kc, sl(t)], start=(kc == 0), stop=(kc == 7))
                for kc in range(8):
                    p.mm(bank(b0 + 1), wg[:, kc, :], XN[:, kc, sl(t)], start=(kc == 0), stop=(kc == 7))
                conv_chunk(bank(b0), xs_pp, t % 2, t == 0, 512, 40 + fc * 4, 72 + fc, 4, xc[:])
                p.copy(xcb[:], xc[:], eng="pool")
                p.mm(bank(b0 + 2), BDb[:, 0, fc, :], xcb[:])
                p.mm(bank(b0 + 3), BDb[:, 1, fc, :], xcb[:])
                p.act(rr[:], bank(b0 + 2), AF.Sigmoid, bias=PV[:, 80 + fc:81 + fc])
                p.act(ii[:], bank(b0 + 3), AF.Sigmoid, bias=PV[:, 88 + fc:89 + fc])
                p.act(aa[:], rr[:], AF.Exp, scale=CL[:, fc:fc + 1])
                p.tt(a2[:], aa[:], aa[:], ALU.mult)
                p.act(a2[:], a2[:], AF.Sqrt, bias=1.0, scale=-1.0)
                p.tt(ii[:], ii[:], xc[:], ALU.mult, eng="pool")
                p.tt(ii[:], ii[:], a2[:], ALU.mult)
                h_ = hs[t % 2]
                init = 0.0 if t == 0 else hs[1 - t % 2][:, 511:512]
                p.scan(h_[:], aa[:], ii[:], init, ALU.mult, ALU.add)
                gelu_mul(YA[:, fc, sl(t)], bank(b0 + 1), h_[:], g1[:], last_eng="pool")
        if debug == "ya" and l == 0:
            break
        ar.reset(mA1)
        proj_merge(l, YA, wba_d, 48, True)
        ar.reset(mA)
        if debug == "hA" and l == 0:
            break

        YBT = ar.alloc("YBT", [128, 8, S], BF16)
        mB = ar.mark()
        ekT = ar.alloc("ekT", [128, 16, 4], F32)
        thrT = ar.alloc("thrT", [128, 16, 4], F32)
        decB = ar.alloc("decB", [128, 4, 16], F32)
        mB2 = ar.mark()
        wifs = ar.alloc("wifs", [128, 64], F32)
        wifb = ar.alloc("wifb", [128, 8, 8], BF16)
        p.dma("sp", wifs[:], wif_d[l], "wif")
        p.copy(wifb[:], wifs[:].rearrange("p (k n) -> p k n", k=8))
        li = ar.alloc("li", [4, S], F32)
        xf = ar.alloc("xf", [4, S], F32)
        t1 = ar.alloc("t1", [4, S], F32)
        t2 = ar.alloc("t2", [4, S], F32)
        ones4 = ar.alloc("ones4", [4, S], F32)
        sm = ar.alloc("sm4", [4, 128], F32)
        smb = ar.alloc("sm4b", [4, 64], BF16)
        for t in range(4):
            for kc in range(8):
                p.mm(ps[0:4, t % 2, :], wifb[:, kc, 0:4], XN[:, kc, sl(t)], start=(kc == 0), stop=(kc == 7))
            for kc in range(8):
                p.mm(ps[0:4, 2 + t % 2, :], wifb[:, kc, 4:8], XN[:, kc, sl(t)], start=(kc == 0), stop=(kc == 7))
            p.act(li[:, sl(t)], ps[0:4, t % 2, :], AF.Identity, bias=PV[0:4, 376:377])
            p.act(xf[:, sl(t)], ps[0:4, 2 + t % 2, :], AF.Identity, bias=PV[0:4, 377:378])
        p.memset(ones4[:], 1.0)
        p.act(t1[:], xf[:], AF.Abs)
        p.act(t1[:], t1[:], AF.Exp, scale=-1.0)
        p.act(t1[:], t1[:], AF.Ln, bias=1.0)
        p.ts(t2[:], xf[:], 0.0, None, ALU.min)
        p.tt(t2[:], t2[:], t1[:], ALU.subtract)
        p.scan(xf[:], ones4[:], t2[:], 0.0, ALU.mult, ALU.add)
        p.tt(li[:], li[:], xf[:], ALU.subtract)
        cm = sm[:, 0:16]
        mt = sm[:, 16:32]
        mprev = sm[:, 32:48]
        dec = sm[:, 48:64]
        r1 = sm[:, 64:80]
        r2 = sm[:, 80:96]
        p.add("dve", lambda e, o=cm, i=li[:].rearrange("p (c n) -> p c n", c=16): e.tensor_reduce(o, i, AX.X, ALU.max),
              reads=[li[:]], writes=[cm])
        p.scan(mt, cm, cm, 0.0, ALU.max, ALU.max)
        p.memset(mprev[:, 0:1], 0.0)
        p.copy(mprev[:, 1:16], mt[:, 0:15])
        p.tt(dec, mprev, mt, ALU.subtract)
        p.act(dec, dec, AF.Exp)
        mtb = bc_last(mt, 128)
        p.tt(t1[:].rearrange("p (c n) -> p c n", c=16), li[:].rearrange("p (c n) -> p c n", c=16), mtb, ALU.subtract)
        p.act(t1[:], t1[:], AF.Exp)
        p.tt(t2[:].rearrange("p (c n) -> p c n", c=16), xf[:].rearrange("p (c n) -> p c n", c=16), mtb, ALU.add)
        p.act(t2[:], t2[:], AF.Exp, scale=-2.0, bias=2.0 * LN16 + float(np.log(EPS)))
        for c in range(16):
            p.tr(ps[:, 4, c * 4:(c + 1) * 4], t1[0:4, c * 128:(c + 1) * 128], CF[0:4, 0:4])
            p.tr(ps[:, 5, c * 4:(c + 1) * 4], t2[0:4, c * 128:(c + 1) * 128], CF[0:4, 0:4])
        p.copy(ekT[:].rearrange("p c h -> p (c h)"), ps[:, 4, 0:64])
        p.copy(thrT[:].rearrange("p c h -> p (c h)"), ps[:, 5, 0:64], eng="act")
        p.copy(smb[:, 0:16], dec)
        p.tt(r1, dec, smb[:, 0:16], ALU.subtract)
        p.copy(smb[:, 16:32], r1)
        p.tt(r2, r1, smb[:, 16:32], ALU.subtract)
        p.copy(smb[:, 32:48], r2)
        for h in range(4):
            for q3 in range(3):
                p.mm(ps[:, 6, h * 16:(h + 1) * 16], selb[0:4, h * 128:(h + 1) * 128], smb[0:4, q3 * 16:(q3 + 1) * 16],
                     start=(q3 == 0), stop=(q3 == 2))
        p.copy(decB[:].rearrange("p h c -> p (h c)"), ps[:, 6, 0:64])
        ar.reset(mB2)
        qT = ar.alloc("qT", [128, 2, S], BF16)
        kT = ar.alloc("kT", [128, 2, S], BF16)
        vh = ar.alloc("vh", [128, 16, 257], BF16)
        og = ar.alloc("og", [128, 16, 256], BF16)
        wpb = WPool(p, ar, "wb", 8, 128, 5)
        xs_pp = [ar.alloc(f"xsb{i}", [128, 515], F32) for i in range(2)]
        xcB2 = [ar.alloc(f"xcB{i}", [128, 512], F32) for i in range(2)]
        Sst = ar.alloc("Sst", [128, 2, 257], F32)
        Cb = ar.alloc("Cb", [128, 2, 257], BF16)
        wk = [ar.alloc(f"wk{i}", [128, 256], BF16) for i in range(2)]
        scm = [ar.alloc(f"scm{i}", [128, 128], BF16) for i in range(2)]
        hn = [ar.alloc(f"hn{i}", [128, 256], F32) for i in range(2)]
        ybk = [ar.alloc(f"ybk{i}", [128, 256], BF16) for i in range(2)]
        sml = [ar.alloc(f"sml{i}", [128, 16], F32) for i in range(2)]
        so2 = [ar.alloc(f"so{i}", [128, 256], F32) for i in range(2)]
        for h in range(4):
            for which in range(2):
                dst = qT if which == 0 else kT
                for dc in range(2):
                    ch = which * 8 + h * 2 + dc
                    w = wpb.load(win_d[l, 16 + ch])
                    for t in range(4):
                        b0 = t % 2
                        for kc in range(8):
                            p.mm(bank(b0), w[:, kc, :], XN[:, kc, sl(t)], start=(kc == 0), stop=(kc == 7))
                        conv_chunk(bank(b0), xs_pp, t % 2, t == 0, 512, 104 + ch * 4, 168 + ch, 4, xcB2[t % 2][:])
                        p.act(dst[:, dc, sl(t)], xcB2[t % 2][:], AF.Silu)
            wv = [wpb.load(win_d[l, 32 + h * 2 + e]) for e in range(2)]
            p.memset(vh[:, :, 256:257], 1.0)
            for c in range(16):
                b0 = 2 + c % 2
                for e in range(2):
                    for kc in range(8):
                        p.mm(ps[:, b0, e * 128:(e + 1) * 128], XN[:, kc, sl(c, 128)], wv[e][:, kc, :],
                             start=(kc == 0), stop=(kc == 7))
                p.copy(vh[:, c, 0:256], ps[:, b0, 0:256], eng="act")
            wo_ = [wpb.load(win_d[l, 40 + h * 2 + e]) for e in range(2)]
            for c in range(16):
                b0 = 2 + c % 2
                for e in range(2):
                    for kc in range(8):
                        p.mm(ps[:, b0, e * 128:(e + 1) * 128], XN[:, kc, sl(c, 128)], wo_[e][:, kc, :],
                             start=(kc == 0), stop=(kc == 7))
                p.act(so2[c % 2][:], ps[:, b0, 0:256], AF.Sigmoid)
                p.tt(og[:, c, :], so2[c % 2][:], GB[:, h * 256:(h + 1) * 256], ALU.mult)
            p.memset(Sst[:], 0.0)
            p.memset(Cb[:], 0.0)
            for c in range(16):
                cs = sl(c, 128)
                ek = ekT[:, c, h:h + 1]
                pb = c % 2
                ib = 5 + pb
                for dc in range(2):
                    p.mm(ps[:, 4, 128:256], kT[:, dc, cs], qT[:, dc, cs], start=(dc == 0), stop=(dc == 1))
                kb = bankb(4)
                for dc in range(2):
                    p.tr(kb[:, dc * 128:(dc + 1) * 128], kT[:, dc, cs], identb[:])
                p.ts(wk[pb][:], kb[:, 0:256], ek, None, ALU.mult)
                p.stt(scm[pb][:], ps[:, 4, 128:256], ek, maskf, ALU.mult, ALU.mult)
                for dc in range(2):
                    p.mm(ps[:, ib, 0:257], qT[:, dc, cs], Cb[:, dc, :], start=(dc == 0), stop=False)
                p.mm(ps[:, ib, 0:257], scm[pb][:], vh[:, c, :], start=False, stop=True)
                for dc in range(2):
                    p.mm(ps[:, 7, dc * 256:dc * 256 + 257] if False else ps[:, (0 if dc == 0 else 1), 0:257],
                         wk[pb][:, dc * 128:(dc + 1) * 128], vh[:, c, :])
                    p.stt(Sst[:, dc, :], Sst[:, dc, :], decB[:, h, c:c + 1], ps[:, (0 if dc == 0 else 1), 0:257],
                          ALU.mult, ALU.add)
                    if c < 15:
                        p.act(Cb[:, dc, :], Sst[:, dc, :], AF.Identity, scale=decB[:, h, c + 1:c + 2])
                s_ = sml[pb]
                p.act(s_[:, 14:15], ps[:, ib, 256:257], AF.Square, scale=float(np.sqrt(EPS)))
                p.add("dve", lambda e, o=s_[:, 2:8], i=ps[:, ib, 0:256]: e.bn_stats(o, i), reads=[ps[:, ib, 0:256]], writes=[s_[:, 2:8]])
                p.add("dve", lambda e, o=s_[:, 8:10], i=s_[:, 2:8]: e.bn_aggr(o, i), reads=[s_[:, 2:8]], writes=[s_[:, 8:10]])
                p.ts(s_[:, 10:11], s_[:, 14:15], thrT[:, c, h:h + 1], s_[:, 9:10], ALU.max, ALU.add)
                p.act(s_[:, 11:12], s_[:, 10:11], AF.Sqrt)
                p.add("dve", lambda e, o=s_[:, 13:14], i=s_[:, 11:12]: e.reciprocal(o, i), reads=[s_[:, 11:12]], writes=[s_[:, 13:14]])
                p.ts(hn[pb][:], ps[:, ib, 0:256], s_[:, 8:9], s_[:, 13:14], ALU.subtract, ALU.mult)
                p.tt(ybk[pb][:], hn[pb][:], og[:, c, :], ALU.mult)
                yb_ = bankb(7)
                for dc in range(2):
                    p.tr(yb_[:, dc * 128:(dc + 1) * 128], ybk[pb][:, dc * 128:(dc + 1) * 128], identb[:])
                p.copy(YBT[:, 2 * h:2 * h + 2, cs], yb_[:, 0:256].rearrange("p (k n) -> p k n", k=2), eng="act")
        ar.reset(mB)
        if debug == "ybt" and l == 0:
            break
        proj_merge(l, YBT, wbb_d, 56, False)
        ar.reset(mA)
        if debug == "hB" and l == 0:
            break

        rmsnorm(8)
        mX = ar.mark()
        kTm = ar.alloc("kTm", [128, 8, MEM], BF16)
        Vm = ar.alloc("Vm", [128, 2, D], BF16)
        QT = ar.alloc("QT", [128, 8, S], BF16)
        AOT = ar.alloc("AOT", [128, 8, S], BF16)
        wpx = WPool(p, ar, "wx", 8, 128, 4)
        pex = [ar.alloc(f"pex{i}", [128, 256], BF16) for i in range(2)]
        pT = [ar.alloc(f"pT{i}", [128, 2, 128], BF16) for i in range(2)]
        ao = [ar.alloc(f"ao{i}", [128, D], BF16) for i in range(2)]
        smx = [ar.alloc(f"smx{i}", [128, 8], F32) for i in range(2)]
        for j in range(8):
            w = wpx.load(wkv_d[l, j])
            for kc in range(8):
                p.mm(ps[:, j % 2, 0:256], w[:, kc, :], MT[:, kc, :], start=(kc == 0), stop=(kc == 7))
            p.copy(kTm[:, j, :], ps[:, j % 2, 0:256], eng="act")
        for j in range(8):
            w = wpx.load(wkv_d[l, 8 + j])
            for mc in range(2):
                for kc in range(8):
                    p.mm(ps[:, 2 + mc, 0:128], MT[:, kc, mc * 128:(mc + 1) * 128], w[:, kc, :], start=(kc == 0), stop=(kc == 7))
                p.copy(Vm[:, mc, j * 128:(j + 1) * 128], ps[:, 2 + mc, 0:128], eng="act" if mc else "dve")
        for j in range(8):
            w = wpx.load(wq_d[l, j])
            for t in range(4):
                b0 = t % 2
                for kc in range(8):
                    p.mm(bank(b0), w[:, kc, :], XN[:, kc, sl(t)], start=(kc == 0), stop=(kc == 7))
                p.copy(QT[:, j, sl(t)], bank(b0), eng="act" if t % 2 else "dve")
        for ti in range(16):
            cs = sl(ti, 128)
            ao_ = ao[ti % 2]
            for h in range(4):
                pb = h % 2
                sb_ = 2 + pb
                for dc in range(2):
                    p.mm(ps[:, sb_, 0:256], QT[:, 2 * h + dc, cs], kTm[:, 2 * h + dc, :], start=(dc == 0), stop=(dc == 1))
                s_ = smx[pb]
                p.add("dve", lambda e, o=s_[:, 0:1], i=ps[:, sb_, 0:256]: e.tensor_reduce(o, i, AX.X, ALU.max),
                      reads=[ps[:, sb_, 0:256]], writes=[s_[:, 0:1]])
                p.ts(s_[:, 1:2], s_[:, 0:1], -1.0 / 16.0, None, ALU.mult)
                p.act(pex[pb][:], ps[:, sb_, 0:256], AF.Exp, bias=s_[:, 1:2], scale=1.0 / 16.0)
                p.add("dve", lambda e, o=s_[:, 2:3], i=pex[pb][:]: e.tensor_reduce(o, i, AX.X, ALU.add),
                      reads=[pex[pb][:]], writes=[s_[:, 2:3]])
                p.add("dve", lambda e, o=s_[:, 3:4], i=s_[:, 2:3]: e.reciprocal(o, i), reads=[s_[:, 2:3]], writes=[s_[:, 3:4]])
                tb = bankb(4 + pb)
                for mc in range(2):
                    p.tr(tb[:, mc * 128:(mc + 1) * 128], pex[pb][:, mc * 128:(mc + 1) * 128], identb[:])
                p.copy(pT[pb][:], tb[:, 0:256].rearrange("p (k n) -> p k n", k=2), eng="act")
                for mc in range(2):
                    p.mm(ps[:, 6 + pb, 0:256], pT[pb][:, mc, :], Vm[:, mc, h * 256:(h + 1) * 256], start=(mc == 0), stop=(mc == 1))
                p.ts(ao_[:, h * 256:(h + 1) * 256], ps[:, 6 + pb, 0:256], s_[:, 3:4], None, ALU.mult)
            for b in range(2):
                tb = bankb(b)
                for k in range(4):
                    kc = b * 4 + k
                    p.tr(tb[:, k * 128:(k + 1) * 128], ao_[:, kc * 128:(kc + 1) * 128], identb[:])
                p.copy(AOT[:, b * 4:(b + 1) * 4, cs], tb[:, 0:512].rearrange("p (k n) -> p k n", k=4), eng="act" if b else "dve")
        out_proj(l, AOT, wo_d, wpx)
        ar.reset(mX)
        if debug == "hX" and l == 0:
            break

        rmsnorm(16)
        mF = ar.mark()
        ACTB = ar.alloc("ACTB", [128, 24, 1024], BF16)
        halo = ar.alloc("halo", [128, 48, 2], F32)
        wpf = WPool(p, ar, "wf", 8, 128, 4)
        xs_f = [ar.alloc(f"xsf{i}", [128, 1026], F32) for i in range(2)]
        accg = [ar.alloc(f"accg{i}", [128, 512], F32) for i in range(2)]
        accu = [ar.alloc(f"accu{i}", [128, 512], F32) for i in range(2)]
        f1 = [ar.alloc(f"f1{i}", [128, 512], F32) for i in range(2)]
        wdb = [ar.alloc(f"wdb{i}", [128, 24, 128], BF16) for i in range(2)]
        npar = 0
        for T in range(2):
            for j in range(24):
                for which in range(2):
                    cidx = which * 24 + j
                    w = wpf.load(wup_d[l, cidx])
                    wc = 184 + cidx * 3
                    xs = xs_f[npar % 2]
                    npar += 1
                    for hf in range(2):
                        b0 = which * 2 + hf + (j % 2) * 4
                        for kc in range(8):
                            p.mm(bank(b0), w[:, kc, :], XN[:, kc, T * 1024 + hf * 512:T * 1024 + (hf + 1) * 512],
                                 start=(kc == 0), stop=(kc == 7))
                        acc = (accg if which == 0 else accu)[hf]
                        o_ = hf * 512
                        p.act(acc[:], bank(b0), AF.Identity, bias=PV[:, 328 + cidx:329 + cidx], scale=PV[:, wc + 2:wc + 3])
                        p.copy(xs[:, 2 + o_:514 + o_], bank(b0), eng="act")
                        if hf == 0:
                            if T == 0:
                                p.memset(xs[:, 0:2], 0.0)
                            else:
                                p.copy(xs[:, 0:2], halo[:, cidx, :])
                        elif T == 0:
                            p.copy(halo[:, cidx, :], xs[:, 1024:1026])
                        p.stt(acc[:], xs[:, 1 + o_:513 + o_], PV[:, wc + 1:wc + 2], acc[:], ALU.mult, ALU.add)
                        p.stt(acc[:], xs[:, o_:512 + o_], PV[:, wc:wc + 1], acc[:], ALU.mult, ALU.add)
                for hf in range(2):
                    gelu_mul(ACTB[:, j, sl(hf)], accg[hf][:], accu[hf][:], f1[hf][:], last_eng=FFN_LAST, mid_eng=FFN_LAST)
            for jo in range(8):
                w = wdb[jo % 2]
                for part in range(3):
                    wpf.load(wdn_d[l, jo, :, part * 1024:(part + 1) * 1024], dst=w[:, part * 8:(part + 1) * 8, :],
                             key=f"wdb{jo % 2}_{part}")
                for hf in range(2):
                    b0 = hf
                    for kc in range(24):
                        p.mm(bank(b0), w[:, kc, :], ACTB[:, kc, sl(hf)], start=(kc == 0), stop=(kc == 23))
                    tsl = slice(T * 1024 + hf * 512, T * 1024 + (hf + 1) * 512)
                    p.tt(H[:, jo, tsl], H[:, jo, tsl], bank(b0), ALU.add)
        ar.reset(mF)

    if debug is None:
        rmsnorm_out = True
    mO = ar.mark()
    if debug in ("xn",):
        ar.reset(base)
        tmpf = ar.alloc("tmpf", [128, 8, 512], F32)
        for t in range(4):
            p.copy(tmpf[:], XN[:, :, sl(t)])
            for kc in range(8):
                p.dma("sp", dbg_d[:, kc * 2048 + t * 512:kc * 2048 + (t + 1) * 512], tmpf[:, kc, :], "dbg")
    elif debug in ("ya", "ybt"):
        ar.reset(mA)
        ar.alloc("keep", [128, 8, S], BF16)
        tmpf = ar.alloc("tmpf", [128, 8, 512], F32)
        src = YA if debug == "ya" else YBT
        for t in range(4):
            p.copy(tmpf[:], src[:, :, sl(t)])
            for kc in range(8):
                p.dma("sp", dbg_d[:, kc * 2048 + t * 512:kc * 2048 + (t + 1) * 512], tmpf[:, kc, :], "dbg")
    elif debug is not None:
        for kc in range(8):
            p.dma("sp", dbg_d[:, kc * 2048:(kc + 1) * 2048], H[:, kc, :], "dbg")
    if debug is not None:
        p.finish("sp", [dbg_d])
    ar.reset(base)
    sq = [ar.alloc(f"fsq{i}", [128, 512], BF16) for i in range(2)]
    rt = ar.alloc("frt", [128, 512], F32)
    rs = ar.alloc("frs", [128, 512], F32)
    on = ar.alloc("fon", [128, 8, 512], F32)
    ost = [ar.alloc(f"ost{i}", [128, D], F32) for i in range(2)]
    for t in range(4):
        for kc in range(8):
            s_ = sq[kc % 2]
            p.act(s_[:], H[:, kc, sl(t)], AF.Square)
            p.mm(bank(t), onesb[:], s_[:], start=(kc == 0), stop=(kc == 7))
        p.act(rt[:], bank(t), AF.Sqrt, bias=EPS, scale=1.0 / D)
        p.add("dve", lambda e, o=rs[:], i=rt[:]: e.reciprocal(o, i), reads=[rt[:]], writes=[rs[:]])
        for kc in range(8):
            p.stt(on[:, kc, :], H[:, kc, sl(t)], PV[:, 24 + kc:25 + kc], rs[:], ALU.mult, ALU.mult)
        for q4 in range(4):
            ti = t * 4 + q4
            o_ = ost[ti % 2]
            for b in range(2):
                for k in range(4):
                    kc = b * 4 + k
                    p.tr(ps[:, 4 + b, k * 128:(k + 1) * 128], on[:, kc, q4 * 128:(q4 + 1) * 128], identf)
                p.copy(o_[:, b * 512:(b + 1) * 512], ps[:, 4 + b, :], eng="act" if b else "dve")
            p.dma("sp", out_d[ti * 128:(ti + 1) * 128, :], o_[:], f"ost{ti % 2}")
    p.finish("sp", [out_d])
    cnt = p.emit()
    return nc, cnt, len(p.ops)


def _tiles(W, ncols=128):
    K, N = W.shape
    kc = K // 128
    nt = N // ncols
    return np.ascontiguousarray(
        W.reshape(kc, 128, nt, ncols).transpose(2, 1, 0, 3)).reshape(nt, 128, kc * ncols)


def _fm(v):
    return np.ascontiguousarray(v.reshape(-1, 128).T)


def prep_inputs(inp):
    f = lambda a: np.asarray(a, dtype=np.float32)
    consts = np.zeros((128, 768), np.float32)
    consts[:, 0:128] = np.eye(128, dtype=np.float32)
    consts[:, 128:256] = np.triu(np.ones((128, 128), np.float32))
    for h in range(4):
        consts[h, 256 + h * 128:256 + (h + 1) * 128] = 1.0
    pvec = np.zeros((L, 128, NP), np.float32)
    gbc = np.zeros((L, 128, D), np.float32)
    bd = np.zeros((L, 2, 128, 8, 128), np.float32)
    sh = {}
    w_in = f(inp["w_in"])
    for l in range(L):
        pv = pvec[l]
        pv[:, 0:8] = _fm(f(inp["norm_mix_g"])[l])
        pv[:, 8:16] = _fm(f(inp["norm_xa_g"])[l])
        pv[:, 16:24] = _fm(f(inp["norm_ffn_g"])[l])
        pv[:, 24:32] = _fm(f(inp["final_norm_g"]))
        pv[:, 32:40] = _fm(f(inp["mem_norm_g"]))
        cw = f(inp["rnn_conv_w"])[l]
        pv[:, 40:72] = np.stack([_fm(cw[j]) for j in range(4)], axis=2).reshape(128, 32)
        pv[:, 72:80] = _fm(f(inp["rnn_conv_b"])[l])
        pv[:, 80:88] = _fm(f(inp["lru_ba"])[l])
        pv[:, 88:96] = _fm(f(inp["lru_bx"])[l])
        pv[:, 96:104] = _fm(f(inp["lru_lambda"])[l])
        mw = f(inp["ml_conv_w"])[l]
        pv[:, 104:168] = np.stack([_fm(mw[j]) for j in range(4)], axis=2).reshape(128, 64)
        pv[:, 168:184] = _fm(f(inp["ml_conv_b"])[l])
        fw_ = f(inp["ffn_conv_w"])[l]
        pv[:, 184:328] = np.stack([_fm(fw_[j]) for j in range(3)], axis=2).reshape(128, 144)
        pv[:, 328:376] = _fm(f(inp["ffn_conv_b"])[l])
        ifb = f(inp["ml_if_b"])[l]
        pv[0:4, 376] = ifb[0:4]
        pv[0:4, 377] = ifb[4:8]
        gbc[l] = np.broadcast_to(f(inp["ml_norm_g"])[l][None, :], (128, D))
        for gi, nm in enumerate(("lru_wa", "lru_wx")):
            wg = f(inp[nm])[l]
            for g in range(16):
                fc, hb = g // 2, g % 2
                bd[l, gi, hb * 64:(hb + 1) * 64, fc, hb * 64:(hb + 1) * 64] = wg[g]
    main = np.concatenate([w_in[:, :, 0:6144], w_in[:, :, 6152:8200]], axis=2)
    sh["w_in_t"] = np.stack([_tiles(main[l]) for l in range(L)])
    wif = w_in[:, :, 6144:6152]
    sh["w_if"] = np.stack([np.ascontiguousarray(wif[l].reshape(8, 128, 8).transpose(1, 0, 2)).reshape(128, 64) for l in range(L)])
    for k_, nm in (("w_ba", "w_branch_a"), ("w_bb", "w_branch_b"), ("w_mo", "w_mix_out"), ("xa_wq", "xa_wq"),
                   ("xa_wkv", "xa_wkv"), ("xa_wo", "xa_wo"), ("ffn_up", "ffn_w_up"), ("ffn_dn", "ffn_w_down")):
        a = f(inp[nm])
        sh[k_] = np.stack([_tiles(a[l]) for l in range(L)])
    sh["consts"] = consts
    sh["pvec"] = pvec
    sh["gbc"] = gbc
    sh["bd"] = bd.reshape(L, 2, 128, 1024)
    return sh


_CACHE = {}


def kernel(**inputs):
    sh = prep_inputs(inputs)
    x = np.asarray(inputs["x"], dtype=np.float32)
    mem = np.asarray(inputs["mem"], dtype=np.float32)
    if "nc" not in _CACHE:
        _CACHE["nc"] = build(None)[0]
    nc = _CACHE["nc"]
    in_maps = []
    for b in range(8):
        m = dict(sh)
        m["x"] = np.ascontiguousarray(x[b])
        m["mem"] = np.ascontiguousarray(mem[b])
        in_maps.append(m)
    res = run_bass_kernel_spmd(nc, in_maps, core_ids=list(range(8)))
    return np.stack([np.asarray(res.results[b]["out"], dtype=np.float32) for b in range(8)])
```

```python
import contextlib
import numpy as np
import concourse.bass as bass
import concourse.mybir as mybir
from concourse.bass_utils import run_bass_kernel_spmd

F32 = mybir.dt.float32
BF16 = mybir.dt.bfloat16
AF = mybir.ActivationFunctionType
ALU = mybir.AluOpType
AX = mybir.AxisListType


def _esize(dt):
    return mybir.dt.size(dt)


class Op:
    __slots__ = ("eng", "fn", "deps", "idx", "signal", "val", "is_dma", "dkey", "nm", "cost", "alldeps", "fcls")

    def __init__(self, eng, fn, idx, is_dma=False, dkey=None, nm=""):
        self.eng = eng
        self.fn = fn
        self.idx = idx
        self.deps = {}
        self.signal = False
        self.val = 0
        self.is_dma = is_dma
        self.dkey = dkey
        self.nm = nm
        self.cost = 100.0
        self.alldeps = None
        self.fcls = None


class Prog:
    ENGS = ("pe", "act", "dve", "pool", "sp")

    def __init__(self, nc):
        self.nc = nc
        self.ops = []
        self.recs = {}
        self.dma_cnt = {}
        self.sb_lo = 16512
        self.sb_hi = 229344
        self.names = 0

    def sbuf(self, name, shape, dtype, offset):
        self.names += 1
        nbytes = int(np.prod(shape[1:])) * _esize(dtype)
        assert offset % 32 == 0, (name, offset)
        assert self.sb_lo + offset + nbytes <= self.sb_hi, (name, offset, nbytes)
        return self.nc.alloc_sbuf_tensor_at(
            f"{name}_{self.names}", list(shape), dtype, offset=self.sb_lo + offset)

    def _foot(self, ap):
        t = ap.tensor
        es = _esize(ap.dtype)
        dims = ap.ap
        space = str(ap.space)
        off = ap.offset
        if "SB" in space or "PSUM" in space:
            pstep, pcnt = dims[0]
            if pstep > 0:
                p0 = off // pstep
                foff = off % pstep
            else:
                p0 = 0
                foff = off
            if "SB" in space:
                base = t.manual_sbuf_range[0]
                key = "SB"
            else:
                base = 0
                key = "PS"
            lo = base + foff * es
            if key == "PS":
                hi = lo
                for s_, c_ in dims[1:]:
                    hi += (c_ - 1) * abs(s_) * es
                b_lo = lo // 2048
                b_hi = hi // 2048
                return [("PS", 0, 128, b * 2048, (b + 1) * 2048) for b in range(b_lo, b_hi + 1)]
            free = [d for d in dims[1:] if d[1] > 1]
            if not free:
                return [(key, p0, p0 + pcnt, lo, lo + es)]
            inner = free[-1]
            ilen = ((inner[1] - 1) * abs(inner[0]) + 1) * es
            outer = free[:-1]
            nout = 1
            for s, c in outer:
                nout *= c
            if nout > 64 or any(s < 0 for s, c in free):
                hi = lo
                for s, c in free:
                    hi += (c - 1) * abs(s) * es
                return [(key, p0, p0 + pcnt, lo, hi + es)]
            res = []
            idxs = [0] * len(outer)
            while True:
                o = lo
                for (s, c), i in zip(outer, idxs):
                    o += i * s * es
                res.append((key, p0, p0 + pcnt, o, o + ilen))
                k = len(outer) - 1
                while k >= 0:
                    idxs[k] += 1
                    if idxs[k] < outer[k][1]:
                        break
                    idxs[k] = 0
                    k -= 1
                if k < 0:
                    break
            return res
        else:
            lo = off * es
            hi = lo
            for s, c in dims:
                hi += (c - 1) * abs(s) * es
            return [("D:" + t.name, 0, 1, lo, hi + es)]

    def add(self, eng, fn, reads=(), writes=(), dkey=None, nm=""):
        idx = len(self.ops)
        is_dma = dkey is not None
        op = Op(eng, fn, idx, is_dma, dkey, nm)
        ekey = ("dma", idx) if is_dma else eng
        newrecs = []
        for ap in reads:
            for (key, p0, p1, lo, hi) in self._foot(ap):
                d = self.recs.setdefault(key, {})
                if key == "PS":
                    dead = []
                    for rk, oi in d.items():
                        (q0, q1, l2, h2, ek, isw, tw) = rk
                        if p0 < q1 and q0 < p1 and lo < h2 and l2 < hi:
                            if tw:
                                op.deps[oi] = "raw"
                            elif oi not in op.deps:
                                op.deps[oi] = "war"
                            dead.append(rk)
                    for rk in dead:
                        del d[rk]
                    newrecs.append((key, (p0, p1, lo, hi, ekey, True, False)))
                    continue
                for (q0, q1, l2, h2, ek, isw, tw), oi in d.items():
                    if isw and p0 < q1 and q0 < p1 and lo < h2 and l2 < hi:
                        op.deps[oi] = "raw"
                newrecs.append((key, (p0, p1, lo, hi, ekey, False, False)))
        for ap in writes:
            for (key, p0, p1, lo, hi) in self._foot(ap):
                d = self.recs.setdefault(key, {})
                dead = []
                for rk, oi in d.items():
                    (q0, q1, l2, h2, ek, isw, tw) = rk
                    if p0 < q1 and q0 < p1 and lo < h2 and l2 < hi:
                        if oi not in op.deps:
                            op.deps[oi] = "waw" if tw else "war"
                        if p0 <= q0 and q1 <= p1 and lo <= l2 and h2 <= hi:
                            dead.append(rk)
                for rk in dead:
                    del d[rk]
                newrecs.append((key, (p0, p1, lo, hi, ekey, True, True)))
        for key, rk in newrecs:
            prev = self.recs[key].get(rk)
            if prev is not None and prev != idx and prev not in op.deps:
                op.deps[prev] = "ord"
            self.recs[key][rk] = idx
        if is_dma:
            self.dma_cnt[dkey] = self.dma_cnt.get(dkey, 0) + 16
            op.val = self.dma_cnt[dkey]
        try:
            if eng == "pe":
                n = reads[1].free_size() if len(reads) > 1 else 128
                op.cost = max(64.0, float(n)) / 2.0 + 12.0
            elif is_dma:
                a = writes[0]
                op.cost = 2000.0 + a.partition_size() * a.free_size() * _esize(a.dtype) / 150.0
            elif writes:
                a = writes[0]
                op.cost = 120.0 + a.free_size() / 0.96
                if reads and "PSUM" in str(reads[0].space):
                    op.cost += 60.0
                if nm == "scan":
                    op.cost += a.free_size() / 0.96
        except Exception:
            pass
        self.ops.append(op)
        return op

    def mm(self, out, lhsT, rhs, start=True, stop=True):
        return self.add("pe", lambda e: e.matmul(out, lhsT, rhs, start=start, stop=stop),
                        reads=[lhsT, rhs], writes=[out])

    def tr(self, out, in_, ident):
        return self.add("pe", lambda e: e.transpose(out, in_, ident),
                        reads=[in_, ident], writes=[out])

    def act(self, out, in_, func, bias=None, scale=None, eng="act"):
        kw = {}
        rd = [in_]
        if bias is not None:
            kw["bias"] = bias
            if not isinstance(bias, (int, float)):
                rd.append(bias)
        if scale is not None:
            kw["scale"] = scale
            if not isinstance(scale, (int, float)):
                rd.append(scale)
        op = self.add(eng, lambda e: e.activation(out, in_, func, **kw), reads=rd, writes=[out])
        if func in (AF.Sigmoid, AF.Silu, AF.Sqrt):
            op.fcls = str(func)
        elif func in (AF.Exp, AF.Ln):
            op.fcls = "explog"
        return op

    def ts(self, out, in0, s1, s2, op0, op1=None, eng="dve"):
        rd = [in0]
        for s in (s1, s2):
            if s is not None and not isinstance(s, (int, float)):
                rd.append(s)
        if op1 is None:
            return self.add(eng, lambda e: e.tensor_scalar(out, in0, s1, None, op0), reads=rd, writes=[out])
        return self.add(eng, lambda e: e.tensor_scalar(out, in0, s1, s2, op0, op1), reads=rd, writes=[out])

    def tt(self, out, in0, in1, op, eng="dve"):
        return self.add(eng, lambda e: e.tensor_tensor(out, in0, in1, op), reads=[in0, in1], writes=[out])

    def stt(self, out, in0, scalar, in1, op0, op1, eng="dve"):
        rd = [in0, in1]
        if not isinstance(scalar, (int, float)):
            rd.append(scalar)
        return self.add(eng, lambda e: e.scalar_tensor_tensor(out, in0, scalar, in1, op0, op1),
                        reads=rd, writes=[out])

    def copy(self, out, in_, eng="dve"):
        if eng == "act":
            return self.act(out, in_, AF.Copy)
        return self.add(eng, lambda e: e.tensor_copy(out, in_), reads=[in_], writes=[out])

    def memset(self, out, val, eng="dve"):
        return self.add(eng, lambda e: e.memset(out, val), reads=[], writes=[out])

    def scan(self, out, d0, d1, initial, op0, op1):
        rd = [d0, d1]
        if not isinstance(initial, (int, float)):
            rd.append(initial)
        return self.add("dve", lambda e: e.tensor_tensor_scan(out, d0, d1, initial, op0, op1),
                        reads=rd, writes=[out], nm="scan")

    def dma(self, q, out, in_, dkey):
        return self.add(q, lambda e: e.dma_start(out=out, in_=in_), reads=[in_], writes=[out], dkey=dkey)

    def finish(self, eng, aps):
        return self.add(eng, None, reads=list(aps), writes=[])

    def schedule(self):
        import heapq
        ops = self.ops
        n = len(ops)
        succ = [[] for _ in range(n)]
        indeg = [0] * n
        for op in ops:
            op.alldeps = dict(op.deps)
            for d in op.deps:
                succ[d].append(op.idx)
                indeg[op.idx] += 1
        fin = [0.0] * n
        ready = [0.0] * n
        tail = [0.0] * n
        for op in reversed(ops):
            m_ = 0.0
            for sidx in succ[op.idx]:
                if tail[sidx] > m_:
                    m_ = tail[sidx]
            tail[op.idx] = op.cost + 300.0 + m_
        released = {e: [] for e in self.ENGS}
        for op in ops:
            if indeg[op.idx] == 0:
                released[op.eng].append(op.idx)
        tnow = {e: 0.0 for e in self.ENGS}
        order = {e: [] for e in self.ENGS}
        last_tbl = [None]
        remaining = n
        active = set(self.ENGS)
        while remaining > 0:
            cand = [e for e in self.ENGS if released[e]]
            assert cand, "scheduler deadlock"
            e = min(cand, key=lambda k: tnow[k])
            t = tnow[e]
            rl = released[e]
            best = None
            bk = None
            for i in rl:
                pen = 0.0
                if e == "act":
                    fc_ = ops[i].fcls
                    if fc_ is not None and fc_ != last_tbl[0]:
                        pen = 1300.0
                k = (int((max(ready[i], t) + pen) // 250.0), -tail[i], i)
                if bk is None or k < bk:
                    bk = k
                    best = i
            rl.remove(best)
            op = ops[best]
            st_ = max(ready[best], t)
            if e == "act" and op.fcls is not None:
                if op.fcls != last_tbl[0]:
                    st_ += 1300.0
                last_tbl[0] = op.fcls
            if op.is_dma:
                issue = 1500.0 if e == "pool" else 100.0
                tnow[e] = st_ + issue
                fin[best] = st_ + op.cost
            elif op.fn is None:
                tnow[e] = st_
                fin[best] = st_
            else:
                tnow[e] = st_ + op.cost
                fin[best] = st_ + op.cost + (60.0 if e == "pe" else 350.0)
            order[e].append(op)
            remaining -= 1
            for sidx in succ[best]:
                indeg[sidx] -= 1
                if fin[best] > ready[sidx]:
                    ready[sidx] = fin[best]
                if indeg[sidx] == 0:
                    released[ops[sidx].eng].append(sidx)
        self.est_ns = max(tnow.values())
        return order

    def emit(self, sched=True):
        nc = self.nc
        ops = self.ops
        if sched:
            per = self.schedule()
        else:
            per = {e: [op for op in ops if op.eng == e] for e in self.ENGS}
        for op in ops:
            need = {}
            for di, kind in op.deps.items():
                p = ops[di]
                if p.is_dma:
                    need[di] = kind
                elif p.eng == op.eng and not op.is_dma:
                    if op.eng == "pe":
                        continue
                    if kind in ("raw", "war", "waw"):
                        need[di] = kind
                else:
                    need[di] = kind
            op.deps = need
            for di in need:
                if not ops[di].is_dma:
                    ops[di].signal = True
        cnt = {e: 0 for e in self.ENGS}
        for e in self.ENGS:
            for op in per[e]:
                if not op.is_dma and op.signal:
                    cnt[e] += 1
                    op.val = cnt[e]
        with contextlib.ExitStack() as st:
            sems = {}
            for e in self.ENGS:
                sems[e] = st.enter_context(nc.semaphore("s_" + e))
            for k in self.dma_cnt:
                sems[("d", k)] = st.enter_context(nc.semaphore("d_" + str(k)))
            def run(ename, eh):
                waited = {}
                for op in per[ename]:
                    w = {}
                    for di in op.deps:
                        p = ops[di]
                        sk = ("d", p.dkey) if p.is_dma else p.eng
                        if p.val > w.get(sk, 0):
                            w[sk] = p.val
                    for sk, v in w.items():
                        if v > waited.get(sk, 0):
                            eh.wait_ge(sems[sk], v)
                            waited[sk] = v
                    if op.fn is None:
                        continue
                    ins = op.fn(eh)
                    if op.is_dma:
                        ins.then_inc(sems[("d", op.dkey)], 16)
                    elif op.signal:
                        ins.then_inc(sems[ename], 1)

            with nc.Block() as block:
                @block.tensor
                def _(e):
                    run("pe", e)

                @block.scalar
                def _(e):
                    run("act", e)

                @block.vector
                def _(e):
                    run("dve", e)

                @block.gpsimd
                def _(e):
                    run("pool", e)

                @block.sync
                def _(e):
                    run("sp", e)
        return cnt


S = 2048
D = 1024
L = 2
MEM = 256
NP = 384
EPS = 1e-6
LN16 = float(np.log(16.0))
FFN_LAST = "pool"
GC = 1.5957691216057308


def bc_last(ap, n):
    return bass.AP(ap.tensor, ap.offset, [list(d) for d in ap.ap] + [[0, n]])


class Arena:
    def __init__(self, p):
        self.p = p
        self.off = 0

    def alloc(self, name, shape, dtype):
        nbytes = int(np.prod(shape[1:])) * _esize(dtype)
        o = self.off
        self.off = (o + nbytes + 31) // 32 * 32
        return self.p.sbuf(name, shape, dtype, o)

    def mark(self):
        return self.off

    def reset(self, m):
        self.off = m


class WPool:
    def __init__(self, p, ar, name, kc, ncols, nslots=4):
        self.p = p
        self.kc = kc
        self.ncols = ncols
        self.bf = [ar.alloc(f"{name}_bf{i}", [128, kc, ncols], BF16) for i in range(nslots)]
        self.name = name
        self.n = 0

    def load(self, src_tile, dst=None, key=None):
        p = self.p
        if dst is None:
            i = self.n
            self.n += 1
            dst = self.bf[i % len(self.bf)][:]
            key = f"{self.name}{i % len(self.bf)}"
        kc = dst.shape[1]
        p.dma("pool", dst, src_tile.rearrange("p (k n) -> p k n", k=kc), key)
        return dst


def build(debug=None):
    nc = bass.Bass("TRN2", target_bir_lowering=False)
    dt_in = lambda n, s: nc.dram_tensor(n, list(s), F32, kind="ExternalInput").ap()
    x_d = dt_in("x", [S, D])
    mem_d = dt_in("mem", [MEM, D])
    const_d = dt_in("consts", [128, 768])
    pvec_d = dt_in("pvec", [L, 128, NP])
    gb_d = dt_in("gbc", [L, 128, D])
    bd_d = dt_in("bd", [L, 2, 128, 8 * 128])
    win_d = dt_in("w_in_t", [L, 64, 128, 8 * 128])
    wif_d = dt_in("w_if", [L, 128, 8 * 8])
    wba_d = dt_in("w_ba", [L, 8, 128, 8 * 128])
    wbb_d = dt_in("w_bb", [L, 8, 128, 8 * 128])
    wmo_d = dt_in("w_mo", [L, 8, 128, 8 * 128])
    wq_d = dt_in("xa_wq", [L, 8, 128, 8 * 128])
    wkv_d = dt_in("xa_wkv", [L, 16, 128, 8 * 128])
    wo_d = dt_in("xa_wo", [L, 8, 128, 8 * 128])
    wup_d = dt_in("ffn_up", [L, 48, 128, 8 * 128])
    wdn_d = dt_in("ffn_dn", [L, 8, 128, 24 * 128])
    out_d = nc.dram_tensor("out", [S, D], F32, kind="ExternalOutput").ap()
    y1_d = nc.dram_tensor("y1s", [8, 128, S], BF16, kind="Internal").ap()
    dbg_d = None
    if debug is not None:
        dbg_d = nc.dram_tensor("dbg", [128, 8 * 2048], F32, kind="ExternalOutput").ap()

    p = Prog(nc)
    ar = Arena(p)
    ps = nc.alloc_psum_tensor("ps", [128, 8, 512], F32)

    def bank(i):
        return ps[:, i % 8, :]

    def bankb(i):
        return ps[:, i % 8, :].bitcast(BF16)

    H = ar.alloc("H", [128, 8, S], F32)
    XN = ar.alloc("XN", [128, 8, S], BF16)
    CF = ar.alloc("CF", [128, 768], F32)
    identb = ar.alloc("identb", [128, 128], BF16)
    onesb = ar.alloc("onesb", [128, 128], BF16)
    selb = ar.alloc("selb", [4, 512], BF16)
    PV = ar.alloc("PV", [128, NP], F32)
    GB = ar.alloc("GB", [128, D], F32)
    BDb = ar.alloc("BDb", [128, 2, 8, 128], BF16)
    MT = ar.alloc("memnT", [128, 8, MEM], BF16)
    CL = ar.alloc("CL", [128, 16], F32)
    identf = CF[:, 0:128]
    maskf = CF[:, 128:256]
    base = ar.mark()

    def sl(t, n=512):
        return slice(t * n, (t + 1) * n)

    p.dma("sp", CF[:], const_d, "c0")
    p.copy(identb[:], CF[:, 0:128])
    p.memset(onesb[:], 1.0)
    p.copy(selb[:], CF[0:4, 256:768])

    m0 = ar.mark()
    xst = [ar.alloc(f"xst{i}", [128, D], F32) for i in range(2)]
    for t in range(16):
        st = xst[t % 2]
        p.dma("sp", st[:], x_d[t * 128:(t + 1) * 128, :], f"xst{t % 2}")
        for b in range(2):
            for k in range(4):
                kc = b * 4 + k
                p.tr(ps[:, (t * 2 + b) % 8, k * 128:(k + 1) * 128], st[:, kc * 128:(kc + 1) * 128], identf)
            src = ps[:, (t * 2 + b) % 8, :].rearrange("p (k n) -> p k n", k=4)
            p.copy(H[:, b * 4:(b + 1) * 4, t * 128:(t + 1) * 128], src, eng="act" if b else "dve")

    def load_params(l):
        p.dma("sp", PV[:], pvec_d[l], "pv")
        p.dma("sp", GB[:], gb_d[l], "gb")
        bst = xst[0]
        for g in range(2):
            p.dma("sp", bst[:], bd_d[l, g], "bdst")
            p.copy(BDb[:, g, :, :], bst[:].rearrange("p (k n) -> p k n", k=8))
        tmp = xst[1]
        p.act(tmp[:, 0:8], PV[:, 96:104], AF.Exp, scale=-1.0)
        p.act(tmp[:, 8:16], tmp[:, 0:8], AF.Ln, bias=1.0)
        p.ts(CL[:, 0:8], tmp[:, 8:16], -8.0, None, ALU.mult)
        p.ts(CL[:, 8:16], tmp[:, 8:16], -16.0, None, ALU.mult)

    load_params(0)
    mst = xst
    scr = ar.alloc("mscr", [128, D], F32)
    small = ar.alloc("msm", [128, 8], F32)
    for mc in range(2):
        st = mst[mc]
        p.dma("sp", st[:], mem_d[mc * 128:(mc + 1) * 128, :], f"xst{mc}")
        p.tt(scr[:], st[:], st[:], ALU.mult)
        p.add("dve", lambda e, o=small[:, 0:1], i=scr[:]: e.tensor_reduce(o, i, AX.X, ALU.add),
              reads=[scr[:]], writes=[small[:, 0:1]])
        p.act(small[:, 1:2], small[:, 0:1], AF.Sqrt, bias=EPS, scale=1.0 / D)
        p.add("dve", lambda e, o=small[:, 2:3], i=small[:, 1:2]: e.reciprocal(o, i),
              reads=[small[:, 1:2]], writes=[small[:, 2:3]])
        p.ts(scr[:], st[:], small[:, 2:3], None, ALU.mult)
        for b in range(2):
            for k in range(4):
                kc = b * 4 + k
                p.tr(ps[:, b, k * 128:(k + 1) * 128], scr[:, kc * 128:(kc + 1) * 128], identf)
            for k in range(4):
                kc = b * 4 + k
                p.ts(MT[:, kc, mc * 128:(mc + 1) * 128], ps[:, b, k * 128:(k + 1) * 128],
                     PV[:, 32 + kc:33 + kc], None, ALU.mult)
    ar.reset(m0)

    def rmsnorm(gcol0):
        m = ar.mark()
        sq = [ar.alloc(f"sq{i}", [128, 512], BF16) for i in range(2)]
        rt = ar.alloc("rt", [128, 512], F32)
        rs = ar.alloc("rs", [128, 512], F32)
        for t in range(4):
            for kc in range(8):
                s_ = sq[kc % 2]
                p.act(s_[:], H[:, kc, sl(t)], AF.Square)
                p.mm(bank(t), onesb[:], s_[:], start=(kc == 0), stop=(kc == 7))
            p.act(rt[:], bank(t), AF.Sqrt, bias=EPS, scale=1.0 / D)
            p.add("dve", lambda e, o=rs[:], i=rt[:]: e.reciprocal(o, i), reads=[rt[:]], writes=[rs[:]])
            for kc in range(8):
                p.stt(XN[:, kc, sl(t)], H[:, kc, sl(t)], PV[:, gcol0 + kc:gcol0 + kc + 1], rs[:],
                      ALU.mult, ALU.mult)
        ar.reset(m)

    def gelu_mul(out, xin, other, tmp, last_eng="dve", mid_eng="dve"):
        t1 = tmp
        p.act(t1, xin, AF.Square, scale=float(np.sqrt(0.044715)))
        p.stt(t1, t1, 1.0, xin, ALU.add, ALU.mult)
        p.act(t1, t1, AF.Sigmoid, scale=GC)
        p.tt(t1, t1, xin, ALU.mult, eng=mid_eng)
        p.tt(out, t1, other, ALU.mult, eng=last_eng)

    def conv_chunk(psb, xs_pp, par, first, nt, wcol0, bcol, ntap, acc):
        hl = ntap - 1
        xs = xs_pp[par]
        p.copy(xs[:, hl:hl + nt], psb, eng="act")
        if first:
            p.memset(xs[:, 0:hl], 0.0)
        else:
            p.copy(xs[:, 0:hl], xs_pp[1 - par][:, nt:nt + hl])
        p.act(acc, psb, AF.Identity, bias=PV[:, bcol:bcol + 1], scale=PV[:, wcol0 + hl:wcol0 + hl + 1])
        for j in range(hl):
            p.stt(acc, xs[:, j:j + nt], PV[:, wcol0 + j:wcol0 + j + 1], acc, ALU.mult, ALU.add)

    def proj_merge(l, src, wb_d, gtile0, first):
        m = ar.mark()
        Y = ar.alloc("Ymerge", [128, 8, S], BF16)
        wp = WPool(p, ar, "wm", 8, 128, 4)
        sg = [ar.alloc(f"sg{i}", [128, 512], F32) for i in range(2)]
        y1b = None
        if not first:
            y1b = [ar.alloc(f"y1b{i}", [128, S], BF16) for i in range(2)]
        for j in range(8):
            wb = wp.load(wb_d[l, j])
            wg = wp.load(win_d[l, gtile0 + j])
            if not first:
                p.dma("sp", y1b[j % 2][:], y1_d[j], f"y1b{j % 2}")
            for t in range(4):
                b0 = (t % 2) * 2
                for kc in range(8):
                    p.mm(bank(b0), wb[:, kc, :], src[:, kc, sl(t)], start=(kc == 0), stop=(kc == 7))
                for kc in range(8):
                    p.mm(bank(b0 + 1), wg[:, kc, :], XN[:, kc, sl(t)], start=(kc == 0), stop=(kc == 7))
                s_ = sg[t % 2]
                p.act(s_[:], bank(b0 + 1), AF.Sigmoid)
                if first:
                    p.tt(Y[:, j, sl(t)], bank(b0), s_[:], ALU.mult)
                else:
                    p.tt(s_[:], bank(b0), s_[:], ALU.mult)
                    p.tt(Y[:, j, sl(t)], s_[:], y1b[j % 2][:, sl(t)], ALU.add)
            if first:
                p.dma("sp", y1_d[j], Y[:, j, :], f"y1w{j}")
        if not first:
            for j in range(8):
                wm = wp.load(wmo_d[l, j])
                for t in range(4):
                    b0 = 4 + (t % 2)
                    for kc in range(8):
                        p.mm(bank(b0), wm[:, kc, :], Y[:, kc, sl(t)], start=(kc == 0), stop=(kc == 7))
                    p.tt(H[:, j, sl(t)], H[:, j, sl(t)], bank(b0), ALU.add)
        ar.reset(m)

    def out_proj(l, src, w_d, wp):
        for j in range(8):
            wm = wp.load(w_d[l, j])
            for t in range(4):
                b0 = 4 + (t % 2)
                for kc in range(8):
                    p.mm(bank(b0), wm[:, kc, :], src[:, kc, sl(t)], start=(kc == 0), stop=(kc == 7))
                p.tt(H[:, j, sl(t)], H[:, j, sl(t)], bank(b0), ALU.add)

    def dump(ap3):
        shp = ap3.shape
        n = int(np.prod(shp[1:]))
        m = ar.mark()
        p.dma("sp", dbg_d[:, 0:n], ap3, "dbg") if ap3.dtype == F32 else None
        ar.reset(m)

    for l in range(L):
        if l > 0:
            load_params(l)
        rmsnorm(0)
        if debug == "xn" and l == 0:
            break
        mA = ar.mark()
        YA = ar.alloc("YA", [128, 8, S], BF16)
        mA1 = ar.mark()
        wp = WPool(p, ar, "wa", 8, 128, 4)
        xsA = [ar.alloc(f"xsA{i}", [128, 3 + S], F32) for i in range(2)]
        def dbl(nm, dt_=F32):
            return [ar.alloc(f"{nm}{i}", [128, 512], dt_) for i in range(2)]
        xc2, xcb2, rr2, ii2, aa2, a22, xg2, g12 = dbl("xc"), dbl("xcb", BF16), dbl("rr"), dbl("ii"), dbl("aa"), dbl("a2"), dbl("xg"), dbl("g1")
        hs = [ar.alloc(f"hs{i}", [128, 512], F32) for i in range(2)]
        for fc in range(8):
            wx = wp.load(win_d[l, fc])
            wg = wp.load(win_d[l, 8 + fc])
            for t in range(4):
                b0 = (t % 2) * 4
                xc, xcb, rr, ii, aa, a2, xg, g1 = (v[t % 2] for v in (xc2, xcb2, rr2, ii2, aa2, a22, xg2, g12))
                for kc in range(8):
                    p.mm(bank(b0), wx[:, kc, :], XN[:, kc, sl(t)], start=(kc == 0), stop=(kc == 7))
                for kc in range(8):
                    p.mm(bank(b0 + 1), wg[:, kc, :], XN[:, kc, sl(t)], start=(kc == 0), stop=(kc == 7))
                xs = xsA[fc % 2]
                if t == 0:
                    p.memset(xs[:, 0:3], 0.0)
                p.copy(xs[:, 3 + t * 512:3 + (t + 1) * 512], bank(b0), eng="act")
                p.act(xc[:], bank(b0), AF.Identity, bias=PV[:, 72 + fc:73 + fc], scale=PV[:, 43 + fc * 4:44 + fc * 4])
                for j3 in range(3):
                    p.stt(xc[:], xs[:, j3 + t * 512:j3 + t * 512 + 512], PV[:, 40 + fc * 4 + j3:41 + fc * 4 + j3], xc[:],
                          ALU.mult, ALU.add)
                p.copy(xcb[:], xc[:], eng="pool")
                p.mm(bank(b0 + 2), BDb[:, 0, fc, :], xcb[:])
                p.mm(bank(b0 + 3), BDb[:, 1, fc, :], xcb[:])
                p.act(rr[:], bank(b0 + 2), AF.Sigmoid, bias=PV[:, 80 + fc:81 + fc])
                p.act(ii[:], bank(b0 + 3), AF.Sigmoid, bias=PV[:, 88 + fc:89 + fc])
                p.act(aa[:], rr[:], AF.Exp, scale=CL[:, fc:fc + 1])
                p.tt(a2[:], aa[:], aa[:], ALU.mult)
                p.act(a2[:], a2[:], AF.Sqrt, bias=1.0, scale=-1.0)
                p.tt(ii[:], ii[:], xc[:], ALU.mult, eng="pool")
                p.tt(ii[:], ii[:], a2[:], ALU.mult)
                h_ = hs[t % 2]
                init = 0.0 if t == 0 else hs[1 - t % 2][:, 511:512]
                p.scan(h_[:], aa[:], ii[:], init, ALU.mult, ALU.add)
                gelu_mul(YA[:, fc, sl(t)], bank(b0 + 1), h_[:], g1[:], last_eng="pool")
        if debug == "ya" and l == 0:
            break
        ar.reset(mA1)
        proj_merge(l, YA, wba_d, 48, True)
        ar.reset(mA)
        if debug == "hA" and l == 0:
            break

        YBT = ar.alloc("YBT", [128, 8, S], BF16)
        mB = ar.mark()
        ekT = ar.alloc("ekT", [128, 16, 4], F32)
        thrT = ar.alloc("thrT", [128, 16, 4], F32)
        decB = ar.alloc("decB", [128, 4, 16], F32)
        mB2 = ar.mark()
        wifs = ar.alloc("wifs", [128, 64], F32)
        wifb = ar.alloc("wifb", [128, 8, 8], BF16)
        p.dma("sp", wifs[:], wif_d[l], "wif")
        p.copy(wifb[:], wifs[:].rearrange("p (k n) -> p k n", k=8))
        li = ar.alloc("li", [4, S], F32)
        xf = ar.alloc("xf", [4, S], F32)
        t1 = ar.alloc("t1", [4, S], F32)
        t2 = ar.alloc("t2", [4, S], F32)
        ones4 = ar.alloc("ones4", [4, S], F32)
        sm = ar.alloc("sm4", [4, 128], F32)
        smb = ar.alloc("sm4b", [4, 64], BF16)
        for t in range(4):
            for kc in range(8):
                p.mm(ps[0:4, t % 2, :], wifb[:, kc, 0:4], XN[:, kc, sl(t)], start=(kc == 0), stop=(kc == 7))
            for kc in range(8):
                p.mm(ps[0:4, 2 + t % 2, :], wifb[:, kc, 4:8], XN[:, kc, sl(t)], start=(kc == 0), stop=(kc == 7))
            p.act(li[:, sl(t)], ps[0:4, t % 2, :], AF.Identity, bias=PV[0:4, 376:377])
            p.act(xf[:, sl(t)], ps[0:4, 2 + t % 2, :], AF.Identity, bias=PV[0:4, 377:378])
        p.memset(ones4[:], 1.0)
        p.act(t1[:], xf[:], AF.Abs)
        p.act(t1[:], t1[:], AF.Exp, scale=-1.0)
        p.act(t1[:], t1[:], AF.Ln, bias=1.0)
        p.ts(t2[:], xf[:], 0.0, None, ALU.min)
        p.tt(t2[:], t2[:], t1[:], ALU.subtract)
        p.scan(xf[:], ones4[:], t2[:], 0.0, ALU.mult, ALU.add)
        p.tt(li[:], li[:], xf[:], ALU.subtract)
        cm = sm[:, 0:16]
        mt = sm[:, 16:32]
        mprev = sm[:, 32:48]
        dec = sm[:, 48:64]
        r1 = sm[:, 64:80]
        r2 = sm[:, 80:96]
        p.add("dve", lambda e, o=cm, i=li[:].rearrange("p (c n) -> p c n", c=16): e.tensor_reduce(o, i, AX.X, ALU.max),
              reads=[li[:]], writes=[cm])
        p.scan(mt, cm, cm, 0.0, ALU.max, ALU.max)
        p.memset(mprev[:, 0:1], 0.0)
        p.copy(mprev[:, 1:16], mt[:, 0:15])
        p.tt(dec, mprev, mt, ALU.subtract)
        p.act(dec, dec, AF.Exp)
        mtb = bc_last(mt, 128)
        p.tt(t1[:].rearrange("p (c n) -> p c n", c=16), li[:].rearrange("p (c n) -> p c n", c=16), mtb, ALU.subtract)
        p.act(t1[:], t1[:], AF.Exp)
        p.tt(t2[:].rearrange("p (c n) -> p c n", c=16), xf[:].rearrange("p (c n) -> p c n", c=16), mtb, ALU.add)
        p.act(t2[:], t2[:], AF.Exp, scale=-2.0, bias=2.0 * LN16 + float(np.log(EPS)))
        for c in range(16):
            p.tr(ps[:, 4, c * 4:(c + 1) * 4], t1[0:4, c * 128:(c + 1) * 128], CF[0:4, 0:4])
            p.tr(ps[:, 5, c * 4:(c + 1) * 4], t2[0:4, c * 128:(c + 1) * 128], CF[0:4, 0:4])
        p.copy(ekT[:].rearrange("p c h -> p (c h)"), ps[:, 4, 0:64])
        p.copy(thrT[:].rearrange("p c h -> p (c h)"), ps[:, 5, 0:64], eng="act")
        p.copy(smb[:, 0:16], dec)
        p.tt(r1, dec, smb[:, 0:16], ALU.subtract)
        p.copy(smb[:, 16:32], r1)
        p.tt(r2, r1, smb[:, 16:32], ALU.subtract)
        p.copy(smb[:, 32:48], r2)
        for h in range(4):
            for q3 in range(3):
                p.mm(ps[:, 6, h * 16:(h + 1) * 16], selb[0:4, h * 128:(h + 1) * 128], smb[0:4, q3 * 16:(q3 + 1) * 16],
                     start=(q3 == 0), stop=(q3 == 2))
        p.copy(decB[:].rearrange("p h c -> p (h c)"), ps[:, 6, 0:64])
        ar.reset(mB2)
        qT = ar.alloc("qT", [128, 2, S], BF16)
        kT = ar.alloc("kT", [128, 2, S], BF16)
        vh = ar.alloc("vh", [128, 16, 257], BF16)
        og = ar.alloc("og", [128, 16, 256], BF16)
        wpb = WPool(p, ar, "wb", 8, 128, 5)
        xs_pp = [ar.alloc(f"xsb{i}", [128, 515], F32) for i in range(2)]
        xcB2 = [ar.alloc(f"xcB{i}", [128, 512], F32) for i in range(2)]
        Sst = ar.alloc("Sst", [128, 2, 257], F32)
        Cb = ar.alloc("Cb", [128, 2, 257], BF16)
        wk = [ar.alloc(f"wk{i}", [128, 256], BF16) for i in range(2)]
        scm = [ar.alloc(f"scm{i}", [128, 128], BF16) for i in range(2)]
        hn = [ar.alloc(f"hn{i}", [128, 256], F32) for i in range(2)]
        ybk = [ar.alloc(f"ybk{i}", [128, 256], BF16) for i in range(2)]
        sml = [ar.alloc(f"sml{i}", [128, 16], F32) for i in range(2)]
        so2 = [ar.alloc(f"so{i}", [128, 256], F32) for i in range(2)]
        for h in range(4):
            for which in range(2):
                dst = qT if which == 0 else kT
                for dc in range(2):
                    ch = which * 8 + h * 2 + dc
                    w = wpb.load(win_d[l, 16 + ch])
                    for t in range(4):
                        b0 = t % 2
                        for kc in range(8):
                            p.mm(bank(b0), w[:, kc, :], XN[:, kc, sl(t)], start=(kc == 0), stop=(kc == 7))
                        conv_chunk(bank(b0), xs_pp, t % 2, t == 0, 512, 104 + ch * 4, 168 + ch, 4, xcB2[t % 2][:])
                        p.act(dst[:, dc, sl(t)], xcB2[t % 2][:], AF.Silu)
            wv = [wpb.load(win_d[l, 32 + h * 2 + e]) for e in range(2)]
            p.memset(vh[:, :, 256:257], 1.0)
            for c in range(16):
                b0 = 2 + c % 2
                for e in range(2):
                    for kc in range(8):
                        p.mm(ps[:, b0, e * 128:(e + 1) * 128], XN[:, kc, sl(c, 128)], wv[e][:, kc, :],
                             start=(kc == 0), stop=(kc == 7))
                p.copy(vh[:, c, 0:256], ps[:, b0, 0:256], eng="act")
            wo_ = [wpb.load(win_d[l, 40 + h * 2 + e]) for e in range(2)]
            for c in range(16):
                b0 = 2 + c % 2
                for e in range(2):
                    for kc in range(8):
                        p.mm(ps[:, b0, e * 128:(e + 1) * 128], XN[:, kc, sl(c, 128)], wo_[e][:, kc, :],
                             start=(kc == 0), stop=(kc == 7))
                p.act(so2[c % 2][:], ps[:, b0, 0:256], AF.Sigmoid)
                p.tt(og[:, c, :], so2[c % 2][:], GB[:, h * 256:(h + 1) * 256], ALU.mult)
            p.memset(Sst[:], 0.0)
            p.memset(Cb[:], 0.0)
            for c in range(16):
                cs = sl(c, 128)
                ek = ekT[:, c, h:h + 1]
                pb = c % 2
                ib = 5 + pb
                for dc in range(2):
                    p.mm(ps[:, 4, 128:256], kT[:, dc, cs], qT[:, dc, cs], start=(dc == 0), stop=(dc == 1))
                kb = bankb(4)
                for dc in range(2):
                    p.tr(kb[:, dc * 128:(dc + 1) * 128], kT[:, dc, cs], identb[:])
                p.ts(wk[pb][:], kb[:, 0:256], ek, None, ALU.mult)
                p.stt(scm[pb][:], ps[:, 4, 128:256], ek, maskf, ALU.mult, ALU.mult)
                for dc in range(2):
                    p.mm(ps[:, ib, 0:257], qT[:, dc, cs], Cb[:, dc, :], start=(dc == 0), stop=False)
                p.mm(ps[:, ib, 0:257], scm[pb][:], vh[:, c, :], start=False, stop=True)
                for dc in range(2):
                    p.mm(ps[:, 7, dc * 256:dc * 256 + 257] if False else ps[:, (0 if dc == 0 else 1), 0:257],
                         wk[pb][:, dc * 128:(dc + 1) * 128], vh[:, c, :])
                    p.stt(Sst[:, dc, :], Sst[:, dc, :], decB[:, h, c:c + 1], ps[:, (0 if dc == 0 else 1), 0:257],
                          ALU.mult, ALU.add)
                    if c < 15:
                        p.act(Cb[:, dc, :], Sst[:, dc, :], AF.Identity, scale=decB[:, h, c + 1:c + 2])
                s_ = sml[pb]
                p.act(s_[:, 14:15], ps[:, ib, 256:257], AF.Square, scale=float(np.sqrt(EPS)))
                p.add("dve", lambda e, o=s_[:, 2:8], i=ps[:, ib, 0:256]: e.bn_stats(o, i), reads=[ps[:, ib, 0:256]], writes=[s_[:, 2:8]])
                p.add("dve", lambda e, o=s_[:, 8:10], i=s_[:, 2:8]: e.bn_aggr(o, i), reads=[s_[:, 2:8]], writes=[s_[:, 8:10]])
                p.ts(s_[:, 10:11], s_[:, 14:15], thrT[:, c, h:h + 1], s_[:, 9:10], ALU.max, ALU.add)
                p.act(s_[:, 11:12], s_[:, 10:11], AF.Sqrt)
                p.add("dve", lambda e, o=s_[:, 13:14], i=s_[:, 11:12]: e.reciprocal(o, i), reads=[s_[:, 11:12]], writes=[s_[:, 13:14]])
                p.ts(hn[pb][:], ps[:, ib, 0:256], s_[:, 8:9], s_[:, 13:14], ALU.subtract, ALU.mult)
                p.tt(ybk[pb][:], hn[pb][:], og[:, c, :], ALU.mult)
                yb_ = bankb(7)
                for dc in range(2):
                    p.tr(yb_[:, dc * 128:(dc + 1) * 128], ybk[pb][:, dc * 128:(dc + 1) * 128], identb[:])
                p.copy(YBT[:, 2 * h:2 * h + 2, cs], yb_[:, 0:256].rearrange("p (k n) -> p k n", k=2), eng="act")
        ar.reset(mB)
        if debug == "ybt" and l == 0:
            break
        proj_merge(l, YBT, wbb_d, 56, False)
        ar.reset(mA)
        if debug == "hB" and l == 0:
            break

        rmsnorm(8)
        mX = ar.mark()
        kTm = ar.alloc("kTm", [128, 8, MEM], BF16)
        Vm = ar.alloc("Vm", [128, 2, D], BF16)
        QT = ar.alloc("QT", [128, 8, S], BF16)
        AOT = ar.alloc("AOT", [128, 8, S], BF16)
        wpx = WPool(p, ar, "wx", 8, 128, 4)
        pex = [ar.alloc(f"pex{i}", [128, 256], BF16) for i in range(2)]
        pT = [ar.alloc(f"pT{i}", [128, 2, 128], BF16) for i in range(2)]
        ao = [ar.alloc(f"ao{i}", [128, D], BF16) for i in range(2)]
        smx = [ar.alloc(f"smx{i}", [128, 8], F32) for i in range(2)]
        for j in range(8):
            w = wpx.load(wkv_d[l, j])
            for kc in range(8):
                p.mm(ps[:, j % 2, 0:256], w[:, kc, :], MT[:, kc, :], start=(kc == 0), stop=(kc == 7))
            p.copy(kTm[:, j, :], ps[:, j % 2, 0:256], eng="act")
        for j in range(8):
            w = wpx.load(wkv_d[l, 8 + j])
            for mc in range(2):
                for kc in range(8):
                    p.mm(ps[:, 2 + mc, 0:128], MT[:, kc, mc * 128:(mc + 1) * 128], w[:, kc, :], start=(kc == 0), stop=(kc == 7))
                p.copy(Vm[:, mc, j * 128:(j + 1) * 128], ps[:, 2 + mc, 0:128], eng="act" if mc else "dve")
        for j in range(8):
            w = wpx.load(wq_d[l, j])
            for t in range(4):
                b0 = t % 2
                for kc in range(8):
                    p.mm(bank(b0), w[:, kc, :], XN[:, kc, sl(t)], start=(kc == 0), stop=(kc == 7))
                p.copy(QT[:, j, sl(t)], bank(b0), eng="act" if t % 2 else "dve")
        for ti in range(16):
            cs = sl(ti, 128)
            ao_ = ao[ti % 2]
            for h in range(4):
                pb = h % 2
                sb_ = 2 + pb
                for dc in range(2):
                    p.mm(ps[:, sb_, 0:256], QT[:, 2 * h + dc, cs], kTm[:, 2 * h + dc, :], start=(dc == 0), stop=(dc == 1))
                s_ = smx[pb]
                p.add("dve", lambda e, o=s_[:, 0:1], i=ps[:, sb_, 0:256]: e.tensor_reduce(o, i, AX.X, ALU.max),
                      reads=[ps[:, sb_, 0:256]], writes=[s_[:, 0:1]])
                p.ts(s_[:, 1:2], s_[:, 0:1], -1.0 / 16.0, None, ALU.mult)
                p.act(pex[pb][:], ps[:, sb_, 0:256], AF.Exp, bias=s_[:, 1:2], scale=1.0 / 16.0)
                p.add("dve", lambda e, o=s_[:, 2:3], i=pex[pb][:]: e.tensor_reduce(o, i, AX.X, ALU.add),
                      reads=[pex[pb][:]], writes=[s_[:, 2:3]])
                p.add("dve", lambda e, o=s_[:, 3:4], i=s_[:, 2:3]: e.reciprocal(o, i), reads=[s_[:, 2:3]], writes=[s_[:, 3:4]])
                tb = bankb(4 + pb)
                for mc in range(2):
                    p.tr(tb[:, mc * 128:(mc + 1) * 128], pex[pb][:, mc * 128:(mc + 1) * 128], identb[:])
                p.copy(pT[pb][:], tb[:, 0:256].rearrange("p (k n) -> p k n", k=2), eng="act")
                for mc in range(2):
                    p.mm(ps[:, 6 + pb, 0:256], pT[pb][:, mc, :], Vm[:, mc, h * 256:(h + 1) * 256], start=(mc == 0), stop=(mc == 1))
                p.ts(ao_[:, h * 256:(h + 1) * 256], ps[:, 6 + pb, 0:256], s_[:, 3:4], None, ALU.mult)
            for b in range(2):
                tb = bankb(b)
                for k in range(4):
                    kc = b * 4 + k
                    p.tr(tb[:, k * 128:(k + 1) * 128], ao_[:, kc * 128:(kc + 1) * 128], identb[:])
                p.copy(AOT[:, b * 4:(b + 1) * 4, cs], tb[:, 0:512].rearrange("p (k n) -> p k n", k=4), eng="act" if b else "dve")
        out_proj(l, AOT, wo_d, wpx)
        ar.reset(mX)
        if debug == "hX" and l == 0:
            break

        rmsnorm(16)
        mF = ar.mark()
        ACTB = ar.alloc("ACTB", [128, 24, 1024], BF16)
        halo = ar.alloc("halo", [128, 48, 2], F32)
        wpf = WPool(p, ar, "wf", 8, 128, 4)
        xs_f = [ar.alloc(f"xsf{i}", [128, 1026], F32) for i in range(2)]
        accg = [ar.alloc(f"accg{i}", [128, 512], F32) for i in range(2)]
        accu = [ar.alloc(f"accu{i}", [128, 512], F32) for i in range(2)]
        f1 = [ar.alloc(f"f1{i}", [128, 512], F32) for i in range(2)]
        wdb = [ar.alloc(f"wdb{i}", [128, 24, 128], BF16) for i in range(2)]
        npar = 0
        for T in range(2):
            for j in range(24):
                for which in range(2):
                    cidx = which * 24 + j
                    w = wpf.load(wup_d[l, cidx])
                    wc = 184 + cidx * 3
                    xs = xs_f[npar % 2]
                    npar += 1
                    for hf in range(2):
                        b0 = which * 2 + hf + (j % 2) * 4
                        for kc in range(8):
                            p.mm(bank(b0), w[:, kc, :], XN[:, kc, T * 1024 + hf * 512:T * 1024 + (hf + 1) * 512],
                                 start=(kc == 0), stop=(kc == 7))
                        acc = (accg if which == 0 else accu)[hf]
                        o_ = hf * 512
                        p.act(acc[:], bank(b0), AF.Identity, bias=PV[:, 328 + cidx:329 + cidx], scale=PV[:, wc + 2:wc + 3])
                        p.copy(xs[:, 2 + o_:514 + o_], bank(b0), eng="act")
                        if hf == 0:
                            if T == 0:
                                p.memset(xs[:, 0:2], 0.0)
                            else:
                                p.copy(xs[:, 0:2], halo[:, cidx, :])
                        elif T == 0:
                            p.copy(halo[:, cidx, :], xs[:, 1024:1026])
                        p.stt(acc[:], xs[:, 1 + o_:513 + o_], PV[:, wc + 1:wc + 2], acc[:], ALU.mult, ALU.add)
                        p.stt(acc[:], xs[:, o_:512 + o_], PV[:, wc:wc + 1], acc[:], ALU.mult, ALU.add)
                for hf in range(2):
                    gelu_mul(ACTB[:, j, sl(hf)], accg[hf][:], accu[hf][:], f1[hf][:], last_eng=FFN_LAST, mid_eng=FFN_LAST)
            for jo in range(8):
                w = wdb[jo % 2]
                for part in range(3):
                    wpf.load(wdn_d[l, jo, :, part * 1024:(part + 1) * 1024], dst=w[:, part * 8:(part + 1) * 8, :],
                             key=f"wdb{jo % 2}_{part}")
                for hf in range(2):
                    b0 = hf
                    for kc in range(24):
                        p.mm(bank(b0), w[:, kc, :], ACTB[:, kc, sl(hf)], start=(kc == 0), stop=(kc == 23))
                    tsl = slice(T * 1024 + hf * 512, T * 1024 + (hf + 1) * 512)
                    p.tt(H[:, jo, tsl], H[:, jo, tsl], bank(b0), ALU.add)
        ar.reset(mF)

    if debug is None:
        rmsnorm_out = True
    mO = ar.mark()
    if debug in ("xn",):
        ar.reset(base)
        tmpf = ar.alloc("tmpf", [128, 8, 512], F32)
        for t in range(4):
            p.copy(tmpf[:], XN[:, :, sl(t)])
            for kc in range(8):
                p.dma("sp", dbg_d[:, kc * 2048 + t * 512:kc * 2048 + (t + 1) * 512], tmpf[:, kc, :], "dbg")
    elif debug in ("ya", "ybt"):
        ar.reset(mA)
        ar.alloc("keep", [128, 8, S], BF16)
        tmpf = ar.alloc("tmpf", [128, 8, 512], F32)
        src = YA if debug == "ya" else YBT
        for t in range(4):
            p.copy(tmpf[:], src[:, :, sl(t)])
            for kc in range(8):
                p.dma("sp", dbg_d[:, kc * 2048 + t * 512:kc * 2048 + (t + 1) * 512], tmpf[:, kc, :], "dbg")
    elif debug is not None:
        for kc in range(8):
            p.dma("sp", dbg_d[:, kc * 2048:(kc + 1) * 2048], H[:, kc, :], "dbg")
    if debug is not None:
        p.finish("sp", [dbg_d])
    ar.reset(base)
    sq = [ar.alloc(f"fsq{i}", [128, 512], BF16) for i in range(2)]
    rt = ar.alloc("frt", [128, 512], F32)
    rs = ar.alloc("frs", [128, 512], F32)
    on = ar.alloc("fon", [128, 8, 512], F32)
    ost = [ar.alloc(f"ost{i}", [128, D], F32) for i in range(2)]
    for t in range(4):
        for kc in range(8):
            s_ = sq[kc % 2]
            p.act(s_[:], H[:, kc, sl(t)], AF.Square)
            p.mm(bank(t), onesb[:], s_[:], start=(kc == 0), stop=(kc == 7))
        p.act(rt[:], bank(t), AF.Sqrt, bias=EPS, scale=1.0 / D)
        p.add("dve", lambda e, o=rs[:], i=rt[:]: e.reciprocal(o, i), reads=[rt[:]], writes=[rs[:]])
        for kc in range(8):
            p.stt(on[:, kc, :], H[:, kc, sl(t)], PV[:, 24 + kc:25 + kc], rs[:], ALU.mult, ALU.mult)
        for q4 in range(4):
            ti = t * 4 + q4
            o_ = ost[ti % 2]
            for b in range(2):
                for k in range(4):
                    kc = b * 4 + k
                    p.tr(ps[:, 4 + b, k * 128:(k + 1) * 128], on[:, kc, q4 * 128:(q4 + 1) * 128], identf)
                p.copy(o_[:, b * 512:(b + 1) * 512], ps[:, 4 + b, :], eng="act" if b else "dve")
            p.dma("sp", out_d[ti * 128:(ti + 1) * 128, :], o_[:], f"ost{ti % 2}")
    p.finish("sp", [out_d])
    cnt = p.emit()
    return nc, cnt, len(p.ops)


def _tiles(W, ncols=128):
    K, N = W.shape
    kc = K // 128
    nt = N // ncols
    return np.ascontiguousarray(
        W.reshape(kc, 128, nt, ncols).transpose(2, 1, 0, 3)).reshape(nt, 128, kc * ncols)


def _fm(v):
    return np.ascontiguousarray(v.reshape(-1, 128).T)


def prep_inputs(inp):
    f = lambda a: np.asarray(a, dtype=np.float32)
    consts = np.zeros((128, 768), np.float32)
    consts[:, 0:128] = np.eye(128, dtype=np.float32)
    consts[:, 128:256] = np.triu(np.ones((128, 128), np.float32))
    for h in range(4):
        consts[h, 256 + h * 128:256 + (h + 1) * 128] = 1.0
    pvec = np.zeros((L, 128, NP), np.float32)
    gbc = np.zeros((L, 128, D), np.float32)
    bd = np.zeros((L, 2, 128, 8, 128), np.float32)
    sh = {}
    w_in = f(inp["w_in"])
    for l in range(L):
        pv = pvec[l]
        pv[:, 0:8] = _fm(f(inp["norm_mix_g"])[l])
        pv[:, 8:16] = _fm(f(inp["norm_xa_g"])[l])
        pv[:, 16:24] = _fm(f(inp["norm_ffn_g"])[l])
        pv[:, 24:32] = _fm(f(inp["final_norm_g"]))
        pv[:, 32:40] = _fm(f(inp["mem_norm_g"]))
        cw = f(inp["rnn_conv_w"])[l]
        pv[:, 40:72] = np.stack([_fm(cw[j]) for j in range(4)], axis=2).reshape(128, 32)
        pv[:, 72:80] = _fm(f(inp["rnn_conv_b"])[l])
        pv[:, 80:88] = _fm(f(inp["lru_ba"])[l])
        pv[:, 88:96] = _fm(f(inp["lru_bx"])[l])
        pv[:, 96:104] = _fm(f(inp["lru_lambda"])[l])
        mw = f(inp["ml_conv_w"])[l]
        pv[:, 104:168] = np.stack([_fm(mw[j]) for j in range(4)], axis=2).reshape(128, 64)
        pv[:, 168:184] = _fm(f(inp["ml_conv_b"])[l])
        fw_ = f(inp["ffn_conv_w"])[l]
        pv[:, 184:328] = np.stack([_fm(fw_[j]) for j in range(3)], axis=2).reshape(128, 144)
        pv[:, 328:376] = _fm(f(inp["ffn_conv_b"])[l])
        ifb = f(inp["ml_if_b"])[l]
        pv[0:4, 376] = ifb[0:4]
        pv[0:4, 377] = ifb[4:8]
        gbc[l] = np.broadcast_to(f(inp["ml_norm_g"])[l][None, :], (128, D))
        for gi, nm in enumerate(("lru_wa", "lru_wx")):
            wg = f(inp[nm])[l]
            for g in range(16):
                fc, hb = g // 2, g % 2
                bd[l, gi, hb * 64:(hb + 1) * 64, fc, hb * 64:(hb + 1) * 64] = wg[g]
    main = np.concatenate([w_in[:, :, 0:6144], w_in[:, :, 6152:8200]], axis=2)
    sh["w_in_t"] = np.stack([_tiles(main[l]) for l in range(L)])
    wif = w_in[:, :, 6144:6152]
    sh["w_if"] = np.stack([np.ascontiguousarray(wif[l].reshape(8, 128, 8).transpose(1, 0, 2)).reshape(128, 64) for l in range(L)])
    for k_, nm in (("w_ba", "w_branch_a"), ("w_bb", "w_branch_b"), ("w_mo", "w_mix_out"), ("xa_wq", "xa_wq"),
                   ("xa_wkv", "xa_wkv"), ("xa_wo", "xa_wo"), ("ffn_up", "ffn_w_up"), ("ffn_dn", "ffn_w_down")):
        a = f(inp[nm])
        sh[k_] = np.stack([_tiles(a[l]) for l in range(L)])
    sh["consts"] = consts
    sh["pvec"] = pvec
    sh["gbc"] = gbc
    sh["bd"] = bd.reshape(L, 2, 128, 1024)
    return sh


_CACHE = {}


def kernel(**inputs):
    sh = prep_inputs(inputs)
    x = np.asarray(inputs["x"], dtype=np.float32)
    mem = np.asarray(inputs["mem"], dtype=np.float32)
    if "nc" not in _CACHE:
        _CACHE["nc"] = build(None)[0]
    nc = _CACHE["nc"]
    in_maps = []
    for b in range(8):
        m = dict(sh)
        m["x"] = np.ascontiguousarray(x[b])
        m["mem"] = np.ascontiguousarray(mem[b])
        in_maps.append(m)
    res = run_bass_kernel_spmd(nc, in_maps, core_ids=list(range(8)))
    return np.stack([np.asarray(res.results[b]["out"], dtype=np.float32) for b in range(8)])
```
